# Optimizing a Trainium2 kernel written in Bass

```python
import math
import jax
import jax.numpy as jnp
from jax import lax
import numpy as np

D_MODEL = 1024
BATCH = 8
SEQ = 8192
DEPTH = 4
DEC_BATCH = 32
DEC_SEQ = 2048
PAST_LEN = 128

D_CONV = D_MODEL
CONV_WIDTH = 3
D_MLSTM = D_MODEL
N_MLSTM_HEADS = 4
HEAD_DIM = D_MLSTM // N_MLSTM_HEADS
CHUNK = 128
N_GATE_SLOTS = 4
D_IN = 4 * D_CONV + 5 * D_MLSTM + N_GATE_SLOTS * N_MLSTM_HEADS + 2 * D_MODEL
EPS = 1e-6
NEG_BIG = -1e30

kernel_name = "hybrid_conv_mlstm_bidir_encoder"


def rms_norm(x, g):
    xf = x.astype(jnp.float32)
    y = xf * lax.rsqrt(jnp.mean(xf * xf, axis=-1, keepdims=True) + EPS)
    return (y * g.astype(jnp.float32)).astype(x.dtype)


def split_combined(p):
    sizes = [D_CONV, D_CONV, D_CONV, D_CONV,
             D_MLSTM, D_MLSTM, D_MLSTM, D_MLSTM, D_MLSTM,
             N_GATE_SLOTS * N_MLSTM_HEADS,
             2 * D_MODEL]
    idx = []
    acc = 0
    for s in sizes[:-1]:
        acc += s
        idx.append(acc)
    return jnp.split(p, idx, axis=-1)


def centred_dwconv3(u, w, b):
    up = jnp.pad(u, ((0, 0), (1, 1), (0, 0)))
    return up[:, :-2] * w[0] + up[:, 1:-1] * w[1] + up[:, 2:] * w[2] + b


def mlstm_chunkwise(q, k, v, i_pre, f_pre):
    bsz, seq, nh, d = q.shape
    nc = seq // CHUNK

    def chunks4(a):
        return a.reshape(bsz, nc, CHUNK, nh, d).transpose(1, 0, 3, 2, 4)

    def chunks3(a):
        return a.reshape(bsz, nc, CHUNK, nh).transpose(1, 0, 3, 2)

    logf = jax.nn.log_sigmoid(f_pre)
    mask = jnp.tril(jnp.ones((CHUNK, CHUNK), dtype=bool))

    def body(carry, inp):
        C, n, m = carry
        qc, kc, vc, ic, lfc = inp
        b = jnp.cumsum(lfc, axis=-1)
        Dlog = b[..., :, None] - b[..., None, :] + ic[..., None, :]
        Dlog = jnp.where(mask, Dlog, NEG_BIG)
        inter = b + m[..., None]
        m_t = jnp.maximum(jnp.max(Dlog, axis=-1), inter)
        Dexp = jnp.exp(Dlog - m_t[..., None])
        inter_w = jnp.exp(inter - m_t)
        S = jnp.einsum('bhtd,bhsd->bhts', qc, kc) * Dexp
        num = jnp.einsum('bhts,bhsd->bhtd', S, vc) + inter_w[..., None] * jnp.einsum('bhtd,bhde->bhte', qc, C)
        den_raw = jnp.sum(S, axis=-1) + inter_w * jnp.einsum('bhtd,bhd->bht', qc, n)
        den = jnp.maximum(jnp.abs(den_raw), jnp.exp(-m_t))
        h = num / den[..., None]
        bL = b[..., -1]
        a = bL[..., None] - b + ic
        m_new = jnp.maximum(bL + m, jnp.max(a, axis=-1))
        w = jnp.exp(a - m_new[..., None])
        decay = jnp.exp(bL + m - m_new)
        C_new = decay[..., None, None] * C + jnp.einsum('bhs,bhsd,bhse->bhde', w, kc, vc)
        n_new = decay[..., None] * n + jnp.einsum('bhs,bhsd->bhd', w, kc)
        return (C_new, n_new, m_new), h

    init = (jnp.zeros((bsz, nh, d, d), jnp.float32),
            jnp.zeros((bsz, nh, d), jnp.float32),
            jnp.zeros((bsz, nh), jnp.float32))
    _, hs = lax.scan(body, init, (chunks4(q), chunks4(k), chunks4(v), chunks3(i_pre), chunks3(logf)))
    return hs.transpose(1, 0, 3, 2, 4).reshape(bsz, seq, nh, d)


def mixer_layer(x, c, w_ada, b_ada, norm_g, w_in, b_gates, conv_w, conv_b, mh_norm_g,
                w_proj_conv, w_proj_mlstm, w_out):
    bsz, seq, _ = x.shape
    mod = c @ w_ada + b_ada
    shift, scale, gate = jnp.split(mod[:, None, :], 3, axis=-1)
    h = rms_norm(x, norm_g) * (1.0 + scale) + shift
    p = h @ w_in
    cb, cc, cx, cz, q, k, v, o, mz, gpre, gmerge = split_combined(p)

    y_conv = cb * centred_dwconv3(cc * cx, conv_w, conv_b) * jax.nn.silu(cz)

    qf = q.astype(jnp.float32).reshape(bsz, seq, N_MLSTM_HEADS, HEAD_DIM)
    kf = k.astype(jnp.float32).reshape(bsz, seq, N_MLSTM_HEADS, HEAD_DIM) * (HEAD_DIM ** -0.5)
    vf = v.astype(jnp.float32).reshape(bsz, seq, N_MLSTM_HEADS, HEAD_DIM)
    g = gpre.astype(jnp.float32).reshape(bsz, seq, N_GATE_SLOTS, N_MLSTM_HEADS) + b_gates.astype(jnp.float32)
    h_fwd = mlstm_chunkwise(qf, kf, vf, g[:, :, 0], g[:, :, 1])
    h_bwd = jnp.flip(mlstm_chunkwise(jnp.flip(qf, 1), jnp.flip(kf, 1), jnp.flip(vf, 1),
                                     jnp.flip(g[:, :, 2], 1), jnp.flip(g[:, :, 3], 1)), 1)
    hm = h_fwd + h_bwd
    hm = hm * lax.rsqrt(jnp.mean(hm * hm, axis=-1, keepdims=True) + EPS)
    hm = hm * mh_norm_g.astype(jnp.float32).reshape(N_MLSTM_HEADS, HEAD_DIM)
    hm = hm.reshape(bsz, seq, D_MLSTM).astype(x.dtype)
    y_m = jax.nn.sigmoid(o) * hm * jax.nn.silu(mz)

    g_conv, g_mlstm = jnp.split(jax.nn.sigmoid(gmerge), 2, axis=-1)
    merged = g_conv * (y_conv @ w_proj_conv) + g_mlstm * (y_m @ w_proj_mlstm)
    return x + gate * (merged @ w_out)


def setup_inputs(seed: int = 0) -> dict:
    key = jax.random.key(seed)
    ks = jax.random.split(key, 20)
    f32 = jnp.float32
    D = D_MODEL
    nrm = lambda k, shape, s: jax.random.normal(k, shape, f32) * s
    forget_base = 3.0 + jnp.linspace(0.0, 3.0, N_MLSTM_HEADS, dtype=f32)
    b_gates = jnp.stack([
        nrm(ks[0], (DEPTH, N_MLSTM_HEADS), 0.1),
        forget_base + nrm(ks[1], (DEPTH, N_MLSTM_HEADS), 0.1),
        nrm(ks[2], (DEPTH, N_MLSTM_HEADS), 0.1),
        forget_base + nrm(ks[3], (DEPTH, N_MLSTM_HEADS), 0.1),
    ], axis=1)
    return {
        "x_prompt": nrm(ks[4], (BATCH, SEQ, D), 1.0),
        "x_sample": nrm(ks[5], (DEC_BATCH, DEC_SEQ, D), 1.0),
        "c_prompt": nrm(ks[6], (BATCH, D), 1.0),
        "c_sample": nrm(ks[7], (DEC_BATCH, D), 1.0),
        "w_ada": nrm(ks[8], (DEPTH, D, 3 * D), 0.1 * D ** -0.5),
        "b_ada": nrm(ks[9], (DEPTH, 3 * D), 0.02),
        "norm_g": 1.0 + nrm(ks[10], (DEPTH, D), 0.02),
        "w_in": nrm(ks[11], (DEPTH, D, D_IN), D ** -0.5),
        "b_gates": b_gates,
        "conv_w": nrm(ks[12], (DEPTH, CONV_WIDTH, D_CONV), CONV_WIDTH ** -0.5),
        "conv_b": nrm(ks[13], (DEPTH, D_CONV), 0.02),
        "mh_norm_g": 1.0 + nrm(ks[14], (DEPTH, D_MLSTM), 0.02),
        "w_proj_conv": nrm(ks[15], (DEPTH, D_CONV, D), D_CONV ** -0.5),
        "w_proj_mlstm": nrm(ks[16], (DEPTH, D_MLSTM, D), D_MLSTM ** -0.5),
        "w_out": nrm(ks[17], (DEPTH, D, D), D ** -0.5),
        "final_norm_g": 1.0 + nrm(ks[18], (D,), 0.02),
    }


def reference(x_prompt, x_sample, c_prompt, c_sample, w_ada, b_ada, norm_g, w_in, b_gates,
              conv_w, conv_b, mh_norm_g, w_proj_conv, w_proj_mlstm, w_out, final_norm_g):
    def trunk(x, c):
        for l in range(DEPTH):
            x = mixer_layer(x, c, w_ada[l], b_ada[l], norm_g[l], w_in[l], b_gates[l], conv_w[l],
                            conv_b[l], mh_norm_g[l], w_proj_conv[l], w_proj_mlstm[l], w_out[l])
        return rms_norm(x, final_norm_g)

    y_prompt = trunk(x_prompt, c_prompt)
    y_sample = trunk(x_sample, c_sample)
    return (y_prompt, y_sample)
```

```python
import contextlib
import numpy as np
import concourse.bass as bass
import concourse.mybir as mybir
from concourse.bass_utils import run_bass_kernel_spmd

F32 = mybir.dt.float32
BF16 = mybir.dt.bfloat16
AF = mybir.ActivationFunctionType
ALU = mybir.AluOpType
AX = mybir.AxisListType

D = 1024
KC = 8
NH = 4
HD = 256
CH = 128
T = 512
NCH = T // CH
EPS = 1e-6
D_IN = 11280
NBLK = 112
B_CB, B_CC, B_CX, B_CZ, B_Q, B_K, B_V, B_O, B_MZ, B_GC, B_GM, B_PC, B_PM, B_WO = (
    0, 8, 16, 24, 32, 40, 48, 56, 64, 72, 80, 88, 96, 104)
NSLOT = 10
PP_BADA, PP_NG, PP_CW, PP_CB, PP_MG, PP_L = 0, 24, 32, 56, 64, 72


class _Rec:
    def __init__(self):
        self.call = None

    def __getattr__(self, name):
        def f(*a, **kw):
            self.call = (name, a, kw)
            return self
        return f


def _record(fn):
    r = _Rec()
    fn(r)
    name, a, kw = r.call
    return lambda e: getattr(e, name)(*a, **kw)


class Sched:
    ENGS = ("pe", "act", "dve", "pool", "sp")

    def __init__(self):
        self.ops = {e: [] for e in self.ENGS}
        self.count = {}
        self.waited = {e: {} for e in self.ENGS}
        self.last_write = {}
        self.readers = {}
        self.bank_last = {}

    def _deps(self, reads, writes, mykey):
        deps = {}

        def add(tok):
            if tok is None:
                return
            k, v = tok
            if k == mykey:
                return
            if deps.get(k, 0) < v:
                deps[k] = v

        for r in reads:
            add(self.last_write.get(r))
        for w in writes:
            add(self.last_write.get(w))
            for k, v in self.readers.get(w, {}).items():
                add((k, v))
        return deps

    def _commit(self, reads, writes, tok):
        k, v = tok
        for r in reads:
            d = self.readers.setdefault(r, {})
            if d.get(k, 0) < v:
                d[k] = v
        for w in writes:
            self.last_write[w] = tok
            self.readers[w] = {}

    def _push(self, eng, deps, fn, inckey, amt):
        waits = []
        wd = self.waited[eng]
        for k, v in deps.items():
            if wd.get(k, 0) < v:
                wd[k] = v
                waits.append((k, v))
        self.ops[eng].append((waits, _record(fn), inckey, amt))

    def op(self, eng, fn, reads=(), writes=()):
        key = eng
        deps = {}

        def add(tok):
            if tok is None:
                return
            k, v = tok
            if deps.get(k, 0) < v:
                deps[k] = v

        for r in reads:
            add(self.last_write.get(r))
        for w in writes:
            lw = self.last_write.get(w)
            if lw is not None and not (lw[0] == key and eng == "pe"):
                add(lw)
            for k, v in self.readers.get(w, {}).items():
                add((k, v))
        if eng == "pe":
            deps.pop("pe", None)
        n = self.count.get(key, 0) + 1
        self.count[key] = n
        banks = {r[1] for r in list(reads) + list(writes) if isinstance(r, tuple) and r and r[0] == "P"}
        for b in banks:
            bl = self.bank_last.setdefault(b, {})
            for k, v in bl.items():
                if k != key:
                    add((k, v))
            bl[key] = n
        self._push(eng, deps, fn, key, 1)
        self._commit(reads, writes, (key, n))

    def dma(self, q, group, fn, reads=(), writes=()):
        key = ("g", group)
        deps = self._deps(reads, writes, key)
        n = self.count.get(key, 0) + 16
        self.count[key] = n
        self._push(q, deps, fn, key, 16)
        self._commit(reads, writes, (key, n))


DBG = {"on": False, "map": {}}


def _build(seq_lens, depth, debug_stage=99):
    nc = bass.Bass("TRN2", target_bir_lowering=False)
    TC = sum(seq_lens)
    S = len(seq_lens)
    L = depth
    LW = max(L, 1)

    def dram(name, shape, dt, kind):
        return nc.dram_tensor(name, list(shape), dt, kind=kind).ap()

    xin = dram("xin", [TC, D], F32, "ExternalInput")
    winb = dram("winb", [LW, NBLK, 128, KC * 128], F32, "ExternalInput")
    wgin = dram("wgin", [LW, 128, KC * 16], F32, "ExternalInput")
    wada = dram("wada", [LW, 128, KC, 3 * D], F32, "ExternalInput")
    cTin = dram("cT", [128, KC * S], F32, "ExternalInput")
    ppin = dram("pp", [128, LW * PP_L], F32, "ExternalInput")
    bgin = dram("bg", [128, LW * 16], F32, "ExternalInput")
    fgin = dram("fg", [128, D], F32, "ExternalInput")
    cstin = dram("cst", [128, 3 * 128], F32, "ExternalInput")
    yout = dram("y", [TC, D], F32, "ExternalOutput")
    xTd = [dram("xTa", [D, TC], F32, "Internal"), dram("xTb", [D, TC], F32, "Internal")]
    hbd = dram("hb", [TC, D], F32, "Internal")
    wbf = dram("wbf", [LW, NBLK, 128, KC * 128], BF16, "Internal")

    if DBG["on"]:
        dbgf = dram("dbgf", [128, 32768], F32, "ExternalOutput")
        dbgb = dram("dbgb", [128, 65536], BF16, "ExternalOutput")
        DBG["map"] = {}
        DBG["off"] = {"f": 0, "b": 0}

    def dump(name, ap, n, reads, bf=False):
        if not DBG["on"] or name in DBG["map"]:
            return
        kind = "b" if bf else "f"
        o = DBG["off"][kind]
        DBG["off"][kind] = o + n
        DBG["map"][name] = (kind, o, n)
        dst = (dbgb if bf else dbgf)[:, o:o + n]
        S_.dma("sp", ("dbg", name), lambda e: e.dma_start(out=dst, in_=ap), reads=reads, writes=[("dbg", name)])

    es = contextlib.ExitStack()
    with es:
        def sb(name, shape, dt):
            return es.enter_context(nc.sbuf_tensor(name, list(shape), dt))

        def psb(name):
            return es.enter_context(nc.psum_tensor(name, [128, 512], F32))

        XW = T + 2
        xt = [sb(f"xt{i}", [128, KC * XW], F32) for i in range(2)]
        sq = sb("sq", [128, KC * XW], BF16)
        rstd = sb("rstd", [128, XW], F32)
        tmpA = [sb(f"tmpA{i}", [128, XW], F32) for i in range(2)]
        hT = sb("hT", [128, KC * XW], BF16)
        hTb = sb("hTb", [128, KC * XW], BF16)
        qT = sb("qT", [128, KC * T], BF16)
        kT = sb("kT", [128, KC * T], BF16)
        vT = sb("vT", [128, KC * T], BF16)
        ymT = sb("ymT", [128, KC * T], BF16)
        GT = sb("GT", [128, KC * T], F32)
        wring = [sb(f"wr{i}", [128, KC * 128], BF16) for i in range(NSLOT)]
        hacc = [sb(f"hacc{i}", [128, D], F32) for i in range(3)]
        ssqe = sb("ssqe", [128, 16], F32)
        Cst = sb("Cst", [128, NH * 2 * 257], F32)
        Cbf = sb("Cbf", [128, NH * 2 * 257], BF16)
        ktok = [sb(f"ktok{i}", [128, D], BF16) for i in range(NCH)]
        vaug = [sb(f"vaug{i}", [128, NH * 257], BF16) for i in range(NCH)]
        kp = [sb(f"kp{i}", [128, HD], BF16) for i in range(2)]
        SD = [sb(f"SD{i}", [128, CH], BF16) for i in range(2)]
        ubuf = [sb(f"ubuf{i}", [128, XW], F32) for i in range(2)]
        cch = sb("cch", [128, 2], F32)
        ct1 = [sb(f"ct1_{i}", [128, T], F32) for i in range(2)]
        csz = [sb(f"csz{i}", [128, T], F32) for i in range(2)]
        sg1 = [sb(f"sg1_{i}", [128, T], F32) for i in range(2)]
        sg2 = [sb(f"sg2_{i}", [128, T], F32) for i in range(2)]
        cst = sb("cst_sb", [128, 3 * 128], F32)
        identb = sb("identb", [128, 128], BF16)
        maskb = sb("maskb", [128, 2 * 128], BF16)
        onesb = sb("onesb", [128, 128], BF16)
        ones4 = sb("ones4", [4, 128], F32)
        epsb = sb("epsb", [128, 1], F32)
        pp = sb("pp_sb", [128, LW * PP_L], F32)
        bg = sb("bg_sb", [128, LW * 16], F32)
        fg = sb("fg_sb", [128, D], F32)
        cT = sb("cT_sb", [128, KC * S], F32)
        wg = sb("wg_sb", [128, LW * KC * 16], BF16)
        modsb = sb("modsb", [128, LW * S * 24], F32)
        gmod = sb("gmod", [128, LW * S * 8], F32)
        gsb = sb("gsb", [128, 8], F32)
        esp = sb("esp", [128, 8], F32)
        er = sb("er", [4, 256], F32)
        mst = sb("mst", [4, 8], F32)
        ddiag = sb("ddiag", [4, 4], F32)
        eaflo = [sb(f"eaflo{i}", [128, 8], F32) for i in range(NCH)]
        decb = [sb(f"decb{i}", [128, 4], F32) for i in range(NCH)]
        den = sb("den", [128, 8], F32)
        ssq = sb("ssq", [128, 8], F32)
        junk = sb("junk", [128, HD], F32)
        ps = [psb(f"ps{i}") for i in range(8)]
        print("SBUF bytes/partition remaining:", nc.sbuf_bytes_remaining)

        S_ = Sched()
        op, dma = S_.op, S_.dma

        def v3(t_, n):
            return t_[:, :].rearrange("p (k n) -> p k n", n=n)

        xt3 = [v3(x_, XW) for x_ in xt]
        sq3 = v3(sq, XW)
        hTv = [v3(hT, XW), v3(hTb, XW)]
        H = {"v": hTv[0]}
        qT3, kT3, vT3, ymT3, GT3 = (v3(t_, T) for t_ in (qT, kT, vT, ymT, GT))
        ycT3, mgT3 = vT3, qT3
        ycT, mgT = vT, qT
        wr3 = [v3(w_, 128) for w_ in wring]
        ident = cst[:, 0:128]
        Umat = [cst[:, 128:256], cst[:, 256:384]]

        def psbf(bank, lo, n):
            return ps[bank][:, :].bitcast(BF16)[:, lo:lo + n]

        dma("sp", "c0", lambda e: e.dma_start(out=cst[:, :], in_=cstin[:, :]), writes=["cst"])
        dma("sp", "c1", lambda e: e.dma_start(out=pp[:, :], in_=ppin[:, :]), writes=["pp"])
        dma("sp", "c2", lambda e: e.dma_start(out=bg[:, :], in_=bgin[:, :]), writes=["bg"])
        dma("sp", "c3", lambda e: e.dma_start(out=fg[:, :], in_=fgin[:, :]), writes=["fg"])
        dma("sp", "c4", lambda e: e.dma_start(out=cT[:, :], in_=cTin[:, :]), writes=["cT"])
        for l in range(L):
            dma("pool", ("c5", l), lambda e, l=l: e.dma_start(
                out=wg[:, l * KC * 16:(l + 1) * KC * 16], in_=wgin[l, :, :]), writes=[("wg", l)])
        op("dve", lambda e: e.tensor_copy(out=identb[:, :], in_=ident), reads=["cst"], writes=["identb"])
        op("dve", lambda e: e.tensor_copy(out=maskb[:, :], in_=cst[:, 128:384]), reads=["cst"], writes=["maskb"])
        op("dve", lambda e: e.memset(onesb[:, :], 1.0 / 1024.0), writes=["onesb"])
        op("dve", lambda e: e.memset(ones4[:, :], 1.0), writes=["ones4"])
        for c_ in range(NCH):
            op("dve", lambda e, c_=c_: e.memset(vaug[c_][:, :], 1.0), writes=[("vaug", c_)])
        op("dve", lambda e: e.memset(epsb[:, :], EPS), writes=["epsb"])

        for l in range(L):
            order = [32, 40, 48] + [b0 for b0 in range(0, NBLK, 8) if b0 not in (32, 40, 48)]
            for b0 in order:
                g = 0 if 32 <= b0 < 56 else 1
                dma("pool", ("wc", l, g), lambda e, l=l, b0=b0: e.dma_start(
                    out=wbf[l, b0:b0 + 8, :, :], in_=winb[l, b0:b0 + 8, :, :]), writes=[("wbf", l, g)])

        wa3 = [x_[:, 0:KC * 512].rearrange("p (k n) -> p k n", n=512) for x_ in xt]
        pi = 0
        for l in range(L):
            for n6 in range(6):
                bi = pi % 2
                pi += 1
                dma("sp" if bi == 0 else "act", ("xt", bi), lambda e, l=l, n6=n6, bi=bi: e.dma_start(
                    out=wa3[bi], in_=wada[l, :, :, n6 * 512:(n6 + 1) * 512]), writes=[("xt", bi)])
                for m in range(4):
                    mb = n6 * 4 + m
                    for k in range(KC):
                        op("pe", lambda e, bi=bi, m=m, k=k: e.matmul(
                            ps[0][:, m * 8:m * 8 + S], lhsT=wa3[bi][:, k, m * 128:(m + 1) * 128],
                            rhs=cT[:, k * S:(k + 1) * S], start=(k == 0), stop=(k == KC - 1)),
                           reads=[("xt", bi), "cT"], writes=[("P", 0, 0)])
                    o0 = (l * S) * 24 + mb
                    op("dve", lambda e, m=m, o0=o0, l=l, mb=mb: e.tensor_scalar(
                        out=modsb[:, o0:o0 + (S - 1) * 24 + 1:24], in0=ps[0][:, m * 8:m * 8 + S],
                        scalar1=pp[:, l * PP_L + PP_BADA + mb:l * PP_L + PP_BADA + mb + 1], scalar2=None,
                        op0=ALU.add), reads=[("P", 0, 0), "pp"], writes=["modsb"])
            for s in range(S):
                o0 = (l * S + s) * 24
                g0 = (l * S + s) * 8
                op("dve", lambda e, o0=o0, g0=g0, l=l: e.scalar_tensor_tensor(
                    out=gmod[:, g0:g0 + 8], in0=modsb[:, o0 + 8:o0 + 16], scalar=1.0,
                    in1=pp[:, l * PP_L + PP_NG:l * PP_L + PP_NG + 8], op0=ALU.add, op1=ALU.mult),
                   reads=["modsb", "pp"], writes=["gmod"])

        ntile_all = TC // T
        for ti in range(ntile_all):
            t0 = ti * T
            bi = pi % 2
            pi += 1
            xv = xt[bi][:, 0:NCH * D].rearrange("p (c d) -> p c d", d=D)
            dma("sp", ("xt", bi), lambda e, xv=xv, t0=t0: e.dma_start(
                out=xv, in_=xin[t0:t0 + T, :].rearrange("(c p) d -> p c d", p=128)), writes=[("xt", bi)])
            ob = ubuf
            for k in range(KC):
                bank = k % 3
                for c in range(NCH):
                    op("pe", lambda e, bank=bank, c=c, k=k, xv=xv: e.transpose(
                        ps[bank][:, c * 128:(c + 1) * 128], xv[:, c, k * 128:(k + 1) * 128], ident),
                       reads=[("xt", bi), "cst"], writes=[("P", bank, 0)])
                oi = k % 2
                eng = "dve" if k % 2 == 0 else "act"
                if eng == "dve":
                    op("dve", lambda e, bank=bank, oi=oi: e.tensor_copy(out=ob[oi][:, 0:T], in_=ps[bank][:, :]),
                       reads=[("P", bank, 0)], writes=[("ubuf", oi)])
                else:
                    op("act", lambda e, bank=bank, oi=oi: e.activation(out=ob[oi][:, 0:T], in_=ps[bank][:, :],
                                                                         func=AF.Copy),
                       reads=[("P", bank, 0)], writes=[("ubuf", oi)])
                dma("sp", ("ubuf", oi), lambda e, oi=oi, k=k, t0=t0: e.dma_start(
                    out=xTd[0][k * 128:(k + 1) * 128, t0:t0 + T], in_=ob[oi][:, 0:T]),
                    reads=[("ubuf", oi)], writes=[("xT", 0, ti, k)])

        seqs = []
        o = 0
        for ln in seq_lens:
            seqs.append((o, ln))
            o += ln

        def tile_blocks(pass_b):
            if not pass_b:
                return list(range(B_Q, B_Q + 24))
            out = list(range(B_Q, B_Q + 24)) + list(range(B_O, B_O + 16))
            for j in range(8):
                out += [B_CC + j, B_CX + j, B_CZ + j, B_CB + j]
            for j in range(8):
                out += [B_GC + j, B_GM + j, B_PC + j, B_PM + j]
            out += list(range(B_WO, B_WO + 8))
            return out

        wseq = []
        for l in range(L):
            for pass_b in (False, True):
                for (s0, ln) in seqs:
                    for _ in range(ln // T):
                        wseq += [(l, b) for b in tile_blocks(pass_b)]
        wstate = {"ld": 0, "use": 0}

        def w_prefetch(upto):
            while wstate["ld"] < min(upto, len(wseq)):
                i = wstate["ld"]
                l, b = wseq[i]
                slot = i % NSLOT
                dma("sp", ("w", slot), lambda e, l=l, b=b, slot=slot: e.dma_start(
                    out=wring[slot][:, :], in_=wbf[l, b, :, :]), reads=[("wbf", l, 0 if 32 <= b < 56 else 1)], writes=[("w", slot)])
                wstate["ld"] += 1

        def w_use(l, b):
            i = wstate["use"]
            assert wseq[i] == (l, b), (i, wseq[i], l, b)
            w_prefetch(i + NSLOT)
            wstate["use"] += 1
            return i % NSLOT

        bigctr = {"n": 0}

        def next_bank():
            b = bigctr["n"] % 3
            bigctr["n"] += 1
            return b

        def proj_block(l, b, rhs_fn, rhs_reads, n=T, extra=None):
            slot = w_use(l, b)
            bank = next_bank()
            for k in range(KC):
                op("pe", lambda e, slot=slot, bank=bank, k=k: e.matmul(
                    ps[bank][:, 0:n], lhsT=wr3[slot][:, k, :], rhs=rhs_fn(k),
                    start=(k == 0), stop=(k == KC - 1)),
                   reads=[("w", slot)] + rhs_reads, writes=[("P", bank, 0)])
            if extra is not None:
                extra(slot)
            return bank

        def load_x(cur, bi, s0, ln, t0):
            first = (t0 == s0)
            last = (t0 + T == s0 + ln)
            lo = t0 - (0 if first else 1)
            hi = t0 + T + (0 if last else 1)
            c0 = 0 if not first else 1
            for k in range(KC):
                pass
            dma("sp", ("xt", bi), lambda e: e.dma_start(
                out=xt3[bi][:, :, c0:c0 + (hi - lo)],
                in_=xTd[cur][:, lo:hi].rearrange("(k p) t -> p k t", p=128)),
                reads=[("xT", cur, tt, k_) for k_ in range(KC) for tt in
                       ([t0 // T] + ([t0 // T - 1] if not first else []) + ([t0 // T + 1] if not last else []))],
                writes=[("xt", bi)])
            if first:
                op("pool", lambda e: e.memset(xt3[bi][:, :, 0:1], 0.0), writes=[("xt", bi)], reads=[("xt", bi)])
            if last:
                op("pool", lambda e: e.memset(xt3[bi][:, :, XW - 1:XW], 0.0), writes=[("xt", bi)],
                   reads=[("xt", bi)])
            return first, last

        def norm_a(bi):
            for k in range(KC):
                op("act", lambda e, k=k: e.activation(out=sq3[:, k, :], in_=xt3[bi][:, k, :], func=AF.Square),
                   reads=[("xt", bi)], writes=[("sq", k)])

        def norm_b(l, si, bi, first, last, halo, hi):
            hv = hTv[hi]
            bank = next_bank()
            for k in range(KC):
                op("pe", lambda e, k=k, bank=bank: e.matmul(
                    ps[bank][:, :], lhsT=onesb[:, :], rhs=sq3[:, k, 1:T + 1], start=(k == 0), stop=(k == KC - 1)),
                   reads=[("sq", k), "onesb"], writes=[("P", bank, 0)])
            op("act", lambda e, bank=bank: e.activation(out=rstd[:, 1:T + 1], in_=ps[bank][:, :], func=AF.Sqrt,
                                                        bias=epsb[:, 0:1], scale=1.0),
               reads=[("P", bank, 0), "epsb"], writes=["rstd"])
            op("dve", lambda e: e.reciprocal(out=rstd[:, 1:T + 1], in_=rstd[:, 1:T + 1]),
               reads=["rstd"], writes=["rstd"])
            if halo:
                bank2 = next_bank()
                for k in range(KC):
                    op("pe", lambda e, k=k: e.matmul(
                        ps[bank2][:, 0:2], lhsT=onesb[:, :], rhs=sq3[:, k, 0:XW:XW - 1],
                        start=(k == 0), stop=(k == KC - 1)),
                       reads=[("sq", k), "onesb"], writes=[("P", bank2, 0)])
                op("act", lambda e: e.activation(out=rstd[:, 0:XW:XW - 1], in_=ps[bank2][:, 0:2], func=AF.Sqrt,
                                                 bias=epsb[:, 0:1], scale=1.0),
                   reads=[("P", bank2, 0), "epsb"], writes=["rstd"])
                op("dve", lambda e: e.reciprocal(out=rstd[:, 0:XW:XW - 1], in_=rstd[:, 0:XW:XW - 1]),
                   reads=["rstd"], writes=["rstd"])
            c_lo, c_hi = (0, XW) if halo else (1, T + 1)
            g0 = (l * S + si) * 8
            o0 = (l * S + si) * 24
            for k in range(KC):
                ti_ = k % 2
                op("dve", lambda e, k=k, ti_=ti_: e.scalar_tensor_tensor(
                    out=tmpA[ti_][:, c_lo:c_hi], in0=xt3[bi][:, k, c_lo:c_hi], scalar=gmod[:, g0 + k:g0 + k + 1],
                    in1=rstd[:, c_lo:c_hi], op0=ALU.mult, op1=ALU.mult),
                   reads=[("xt", bi), "gmod", "rstd"], writes=[("tmpA", ti_)])
                op("act", lambda e, k=k, ti_=ti_: e.activation(
                    out=hv[:, k, c_lo:c_hi], in_=tmpA[ti_][:, c_lo:c_hi], func=AF.Identity,
                    bias=modsb[:, o0 + k:o0 + k + 1], scale=1.0),
                   reads=[("tmpA", ti_), "modsb"], writes=[("hT", hi, k)])
            hall = [("hT", hi, k) for k in range(KC)]
            if halo and first:
                op("pool", lambda e: e.memset(hv[:, :, 0:1], 0.0), reads=hall, writes=hall)
            if halo and last:
                op("pool", lambda e: e.memset(hv[:, :, XW - 1:XW], 0.0), reads=hall, writes=hall)

        hT_all = [("hT", 0, k) for k in range(KC)]

        def set_h(hi):
            H["v"] = hTv[hi]
            hT_all[:] = [("hT", hi, k) for k in range(KC)]

        def rhs_h(k):
            return H["v"][:, k, 1:T + 1]

        GS0 = 260

        def gates_gen(l, dr, c):
            cs = slice(1 + c * CH, 1 + (c + 1) * CH)
            gcol = GS0 + c * 24
            wgo = l * KC * 16
            for k in range(KC):
                op("pe", lambda e, k=k: e.matmul(
                    ps[7][:, gcol:gcol + 8], lhsT=H["v"][:, k, cs],
                    rhs=wg[:, wgo + k * 16 + dr * 8:wgo + k * 16 + dr * 8 + 8],
                    start=(k == 0), stop=(k == KC - 1)),
                   reads=hT_all + [("wg", l)], writes=[("P", 7, ("g", c))])
            op("dve", lambda e: e.tensor_tensor(
                out=gsb[:, :], in0=ps[7][:, gcol:gcol + 8], in1=bg[:, l * 16 + dr * 8:l * 16 + dr * 8 + 8],
                op=ALU.add), reads=[("P", 7, ("g", c)), "bg"], writes=["gsb"])
            op("act", lambda e: e.activation(out=esp[:, 0:4], in_=gsb[:, 4:8], func=AF.Exp, scale=-1.0),
               reads=["gsb"], writes=["esp0"])
            op("act", lambda e: e.activation(out=esp[:, 4:8], in_=esp[:, 0:4], func=AF.Ln, bias=1.0, scale=1.0),
               reads=["esp0"], writes=["esp1"])
            yield
            r1 = 0
            op("pe", lambda e: e.matmul(ps[7][0:4, r1:r1 + 128], lhsT=esp[:, 4:8], rhs=Umat[dr],
                                        start=True, stop=False),
               reads=["esp1", "cst"], writes=[("P", 7, "r1")])
            op("pe", lambda e: e.matmul(ps[7][0:4, r1:r1 + 128], lhsT=gsb[:, 0:4], rhs=ident,
                                        start=False, stop=True),
               reads=["gsb", "cst"], writes=[("P", 7, "r1")])
            op("pe", lambda e: e.matmul(ps[7][0:4, r1 + 128:r1 + 256], lhsT=esp[:, 4:8], rhs=Umat[dr],
                                        start=True, stop=True),
               reads=["esp1", "cst"], writes=[("P", 7, "r1")])
            lastc = r1 + 128 + (127 if dr == 0 else 0)
            op("dve", lambda e: e.tensor_reduce(out=mst[:, 1:2], in_=ps[7][0:4, r1:r1 + 128], axis=AX.X, op=ALU.max),
               reads=[("P", 7, "r1")], writes=["amax"])
            op("dve", lambda e: e.tensor_tensor(out=mst[:, 2:3], in0=mst[:, 1:2], in1=mst[:, 0:1], op=ALU.max),
               reads=["amax", "mstate"], writes=["R"])
            op("dve", lambda e: e.tensor_scalar(out=mst[:, 3:4], in0=mst[:, 2:3], scalar1=-1.0, scalar2=None,
                                                op0=ALU.mult), reads=["R"], writes=["nR"])
            op("dve", lambda e: e.tensor_tensor(out=mst[:, 4:5], in0=mst[:, 0:1], in1=mst[:, 3:4], op=ALU.add),
               reads=["mstate", "nR"], writes=["dlt"])
            op("dve", lambda e: e.scalar_tensor_tensor(
                out=mst[:, 0:1], in0=ps[7][0:4, lastc:lastc + 1], scalar=-1.0, in1=mst[:, 2:3],
                op0=ALU.mult, op1=ALU.add), reads=[("P", 7, "r1"), "R"], writes=["mstate"])
            op("dve", lambda e: e.tensor_scalar(out=ddiag[:, :], in0=cst[0:4, 0:4], scalar1=mst[:, 4:5],
                                                scalar2=None, op0=ALU.mult), reads=["dlt", "cst"], writes=["ddiag"])
            op("act", lambda e: e.activation(out=er[:, 0:128], in_=ps[7][0:4, r1:r1 + 128], func=AF.Exp,
                                             bias=mst[:, 3:4], scale=1.0), reads=[("P", 7, "r1"), "nR"], writes=["er0"])
            op("act", lambda e: e.activation(out=er[:, 128:256], in_=ps[7][0:4, r1 + 128:r1 + 256], func=AF.Exp,
                                             bias=mst[:, 3:4], scale=1.0), reads=[("P", 7, "r1"), "nR"], writes=["er1"])
            yield
            op("pe", lambda e: e.matmul(ps[7][:, gcol + 8:gcol + 12], lhsT=er[:, 0:128], rhs=cst[0:4, 0:4],
                                        start=True, stop=True), reads=["er0", "cst"], writes=[("P", 7, ("g2", c))])
            op("pe", lambda e: e.matmul(ps[7][:, gcol + 12:gcol + 16], lhsT=er[:, 128:256], rhs=cst[0:4, 0:4],
                                        start=True, stop=True), reads=["er1", "cst"], writes=[("P", 7, ("g2", c))])
            op("pe", lambda e: e.matmul(ps[7][:, gcol + 16:gcol + 20], lhsT=ones4[:, :], rhs=ddiag[:, :],
                                        start=True, stop=True), reads=["ones4", "ddiag"], writes=[("P", 7, ("g2", c))])
            op("dve", lambda e: e.tensor_copy(out=eaflo[c][:, :], in_=ps[7][:, gcol + 8:gcol + 16]),
               reads=[("P", 7, ("g2", c))], writes=[("eaflo", c)])
            op("act", lambda e: e.activation(out=decb[c][:, :], in_=ps[7][:, gcol + 16:gcol + 20], func=AF.Exp),
               reads=[("P", 7, ("g2", c))], writes=[("decb", c)])
            yield

        def phase2(l, dr, chunk_order, pass_b):
            def gsteps():
                for c in chunk_order:
                    yield from gates_gen(l, dr, c)
            gs = gsteps()

            def tick():
                try:
                    next(gs)
                except StopIteration:
                    pass

            for j in range(8):
                bank = proj_block(l, B_Q + j, rhs_h, hT_all)
                op("act", lambda e, j=j, bank=bank: e.activation(out=qT3[:, j, :], in_=ps[bank][:, :], func=AF.Copy),
                   reads=[("P", bank, 0)], writes=[("qT", j)])
                tick()
            for j in range(8):
                bank = proj_block(l, B_K + j, rhs_h, hT_all)
                op("dve", lambda e, j=j, bank=bank: e.tensor_scalar(
                    out=kT3[:, j, :], in0=ps[bank][:, :], scalar1=HD ** -0.5, scalar2=None, op0=ALU.mult),
                   reads=[("P", bank, 0)], writes=[("kT", j)])
                tick()
            kT_all = [("kT", k) for k in range(KC)]
            vT_all = [("vT", k) for k in range(KC)]
            for c in chunk_order:
                cs = slice(c * CH, (c + 1) * CH)
                bank = next_bank()
                for j in range(KC):
                    op("pe", lambda e, j=j, bank=bank: e.transpose(
                        psbf(bank, j * 128, 128), kT3[:, j, cs], identb[:, :]),
                       reads=kT_all + ["identb"], writes=[("P", bank, 0)])
                op("act", lambda e, bank=bank: e.activation(out=ktok[c][:, :], in_=psbf(bank, 0, 1024), func=AF.Copy),
                   reads=[("P", bank, 0)], writes=[("ktok", c)])
            for j in range(8):
                bank = proj_block(l, B_V + j, rhs_h, hT_all)
                if j % 2 == 0:
                    op("act", lambda e, j=j, bank=bank: e.activation(out=vT3[:, j, :], in_=ps[bank][:, :],
                                                                         func=AF.Copy),
                       reads=[("P", bank, 0)], writes=[("vT", j)])
                else:
                    op("dve", lambda e, j=j, bank=bank: e.tensor_copy(out=vT3[:, j, :], in_=ps[bank][:, :]),
                       reads=[("P", bank, 0)], writes=[("vT", j)])
                tick()
            for c in chunk_order:
                cs = slice(c * CH, (c + 1) * CH)
                bank = next_bank()
                for j in range(KC):
                    op("pe", lambda e, j=j, bank=bank: e.transpose(
                        psbf(bank, j * 128, 128), vT3[:, j, cs], identb[:, :]),
                       reads=vT_all + ["identb"], writes=[("P", bank, 0)])
                op("dve", lambda e, bank=bank: e.tensor_copy(
                    out=vaug[c][:, :].rearrange("p (h e) -> p h e", e=257)[:, :, 0:HD],
                    in_=psbf(bank, 0, 1024).rearrange("p (h e) -> p h e", e=HD)),
                   reads=[("P", bank, 0)], writes=[("vaug", c)])
            if pass_b:
                for j in range(8):
                    bo = proj_block(l, B_O + j, rhs_h, hT_all)
                    op("act", lambda e, bo=bo, j=j: e.activation(out=GT3[:, j, :], in_=ps[bo][:, :],
                                                                   func=AF.Sigmoid),
                       reads=[("P", bo, 0)], writes=[("GT", j)])
                    tick()
                for j in range(8):
                    bm = proj_block(l, B_MZ + j, rhs_h, hT_all)
                    i2 = j % 2
                    op("act", lambda e, bm=bm, i2=i2: e.activation(out=sg2[i2][:, :], in_=ps[bm][:, :],
                                                                     func=AF.Silu),
                       reads=[("P", bm, 0)], writes=[("sg2", i2)])
                    op("dve", lambda e, j=j, i2=i2: e.scalar_tensor_tensor(
                        out=GT3[:, j, :], in0=GT3[:, j, :],
                        scalar=pp[:, l * PP_L + PP_MG + j:l * PP_L + PP_MG + j + 1], in1=sg2[i2][:, :],
                        op0=ALU.mult, op1=ALU.mult),
                       reads=[("sg2", i2), ("GT", j), "pp"], writes=[("GT", j)])
                    tick()
            for _ in gs:
                pass

        cnt = {"sd": 0, "kp": 0, "np": 0}

        def mlstm_chunk(dr, c, hb_i, add_bwd):
            cs = slice(c * CH, (c + 1) * CH)
            for hd in range(NH):
                for j in range(2):
                    op("pe", lambda e, j=j, hd=hd: e.matmul(
                        ps[3][:, hd * 128:(hd + 1) * 128], lhsT=kT3[:, 2 * hd + j, cs], rhs=qT3[:, 2 * hd + j, cs],
                        start=(j == 0), stop=(j == 1)),
                       reads=[("kT", 2 * hd + j), ("qT", 2 * hd + j)], writes=[("P", 3, hd)])
            st = {}

            def pre(hd):
                di = cnt["sd"] % 2
                cnt["sd"] += 1
                ki = cnt["kp"] % 2
                cnt["kp"] += 1
                nb = 4 + (cnt["np"] % 2)
                cnt["np"] += 1
                st[hd] = (di, ki, nb)
                op("dve", lambda e: e.scalar_tensor_tensor(
                    out=SD[di][:, :], in0=ps[3][:, hd * 128:(hd + 1) * 128], scalar=eaflo[c][:, hd:hd + 1],
                    in1=maskb[:, dr * 128:(dr + 1) * 128], op0=ALU.mult, op1=ALU.mult),
                   reads=[("P", 3, hd), "maskb", ("eaflo", c)], writes=[("SD", di)])
                op("act", lambda e: e.activation(
                    out=kp[ki][:, :], in_=ktok[c][:, hd * HD:(hd + 1) * HD], func=AF.Identity,
                    scale=eaflo[c][:, hd:hd + 1]),
                   reads=[("ktok", c), ("eaflo", c)], writes=[("kp", ki)])
                c0 = hd * 2 * 257
                for j in range(2):
                    op("act", lambda e, j=j: e.activation(
                        out=Cbf[:, c0 + j * 257:c0 + (j + 1) * 257], in_=Cst[:, c0 + j * 257:c0 + (j + 1) * 257],
                        func=AF.Identity, scale=decb[c][:, hd:hd + 1]),
                       reads=[("C", hd, j), ("decb", c)], writes=[("Cbf", hd, j)])

            def mm_n(hd):
                di, ki, nb = st[hd]
                va = vaug[c][:, hd * 257:(hd + 1) * 257]
                c0 = hd * 2 * 257
                op("pe", lambda e: e.matmul(ps[nb][:, 0:257], lhsT=SD[di][:, :], rhs=va, start=True, stop=False),
                   reads=[("SD", di), ("vaug", c)], writes=[("P", nb, "np")])
                for j in range(2):
                    op("pe", lambda e, j=j: e.matmul(
                        ps[nb][:, 0:257], lhsT=qT3[:, 2 * hd + j, cs], rhs=Cbf[:, c0 + j * 257:c0 + (j + 1) * 257],
                        start=False, stop=(j == 1)),
                       reads=[("qT", 2 * hd + j), ("Cbf", hd, j)], writes=[("P", nb, "np")])

            def post(hd):
                di, ki, nb = st[hd]
                op("dve", lambda e: e.tensor_tensor(
                    out=den[:, hd:hd + 1], in0=ps[nb][:, 256:257], in1=eaflo[c][:, 4 + hd:5 + hd], op=ALU.max),
                   reads=[("P", nb, "np"), ("eaflo", c)], writes=[("den0", hd)])
                op("dve", lambda e: e.scalar_tensor_tensor(
                    out=den[:, hd:hd + 1], in0=ps[nb][:, 256:257], scalar=-1.0, in1=den[:, hd:hd + 1],
                    op0=ALU.mult, op1=ALU.max),
                   reads=[("P", nb, "np"), ("den0", hd)], writes=[("den", hd)])
                op("dve", lambda e: e.reciprocal(out=den[:, 4 + hd:5 + hd], in_=den[:, hd:hd + 1]),
                   reads=[("den", hd)], writes=[("rden", hd)])
                hsl = slice(hd * HD, (hd + 1) * HD)
                if add_bwd:
                    op("dve", lambda e: e.scalar_tensor_tensor(
                        out=hacc[hb_i][:, hsl], in0=ps[nb][:, 0:HD], scalar=den[:, 4 + hd:5 + hd],
                        in1=hacc[hb_i][:, hsl], op0=ALU.mult, op1=ALU.add),
                       reads=[("P", nb, "np"), ("rden", hd), ("hacc", hb_i)], writes=[("hacc", hb_i)])
                else:
                    op("act", lambda e: e.activation(
                        out=hacc[hb_i][:, hsl], in_=ps[nb][:, 0:HD], func=AF.Identity, scale=den[:, 4 + hd:5 + hd]),
                       reads=[("P", nb, "np"), ("rden", hd)], writes=[("hacc", hb_i)])

            def mm_u(hd):
                di, ki, nb = st[hd]
                va = vaug[c][:, hd * 257:(hd + 1) * 257]
                for j in range(2):
                    ub = 6 + j
                    op("pe", lambda e, j=j, ub=ub: e.matmul(
                        ps[ub][:, 0:257], lhsT=kp[ki][:, j * 128:(j + 1) * 128], rhs=va, start=True, stop=True),
                       reads=[("kp", ki), ("vaug", c)], writes=[("P", ub, "up")])

            def upd(hd):
                c0 = hd * 2 * 257
                for j in range(2):
                    ub = 6 + j
                    op("dve", lambda e, j=j, ub=ub: e.scalar_tensor_tensor(
                        out=Cst[:, c0 + j * 257:c0 + (j + 1) * 257], in0=Cst[:, c0 + j * 257:c0 + (j + 1) * 257],
                        scalar=decb[c][:, hd:hd + 1], in1=ps[ub][:, 0:257], op0=ALU.mult, op1=ALU.add),
                       reads=[("C", hd, j), ("decb", c), ("P", ub, "up")], writes=[("C", hd, j)])

            pre(0)
            mm_n(0)
            for hd in range(NH):
                if hd + 1 < NH:
                    pre(hd + 1)
                mm_u(hd)
                if hd + 1 < NH:
                    mm_n(hd + 1)
                post(hd)
                upd(hd)

        def seq_reset():
            op("pool", lambda e: e.memset(Cst[:, :], 0.0),
               reads=[("C", h_, j) for h_ in range(NH) for j in range(2)],
               writes=[("C", h_, j) for h_ in range(NH) for j in range(2)])
            op("dve", lambda e: e.memset(mst[:, 0:1], 0.0), reads=["mstate"], writes=["mstate"])

        hbc = {"n": 0}
        cur = 0
        for l in range(L):
            nxt = 1 - cur
            tilesA = []
            for si in reversed(range(S)):
                s0, ln = seqs[si]
                for t0 in reversed(range(s0, s0 + ln, T)):
                    tilesA.append(dict(si=si, s0=s0, ln=ln, t0=t0, newseq=(t0 + T == s0 + ln)))
            tilesB = []
            for si in range(S):
                s0, ln = seqs[si]
                for t0 in range(s0, s0 + ln, T):
                    tilesB.append(dict(si=si, s0=s0, ln=ln, t0=t0, newseq=(t0 == s0)))

            def prep1(tl, idx):
                nonlocal pi
                tl["bi"] = pi % 2
                pi += 1
                tl["hi"] = idx % 2
                tl["first"], tl["last"] = load_x(cur, tl["bi"], tl["s0"], tl["ln"], tl["t0"])
                norm_a(tl["bi"])

            def prep2(tl, halo):
                norm_b(l, tl["si"], tl["bi"], tl["first"], tl["last"], halo, tl["hi"])

            for idx, tl in enumerate(tilesA):
                if idx == 0:
                    prep1(tl, 0)
                    prep2(tl, False)
                set_h(tl["hi"])
                if tl["newseq"]:
                    seq_reset()
                t0 = tl["t0"]
                phase2(l, 1, list(reversed(range(NCH))), False)
                for ci, c in enumerate(reversed(range(NCH))):
                    hb_i = hbc["n"] % 2
                    hbc["n"] += 1
                    mlstm_chunk(1, c, hb_i, add_bwd=False)
                    r0 = t0 + c * CH
                    dma("sp", ("hacc", hb_i), lambda e, hb_i=hb_i, r0=r0: e.dma_start(
                        out=hbd[r0:r0 + CH, :], in_=hacc[hb_i][:, :]),
                        reads=[("hacc", hb_i)], writes=[("hb", r0 // CH)])
                    if idx + 1 < len(tilesA):
                        if ci == 0:
                            prep1(tilesA[idx + 1], idx + 1)
                        if ci == 1:
                            prep2(tilesA[idx + 1], False)
            for idx, tl in enumerate(tilesB):
                if True:
                    if idx == 0:
                        prep1(tl, 0)
                        prep2(tl, True)
                    set_h(tl["hi"])
                    if tl["newseq"]:
                        seq_reset()
                    t0, si, bi = tl["t0"], tl["si"], tl["bi"]
                    phase2(l, 0, list(range(NCH)), True)
                    dump("GT", GT[:, :], KC * T, [("GT", k) for k in range(KC)])
                    for c in range(NCH):
                        hb_i = hbc["n"] % 2
                        hbc["n"] += 1
                        r0 = t0 + c * CH
                        dma("sp", ("hacc", hb_i), lambda e, hb_i=hb_i, r0=r0: e.dma_start(
                            out=hacc[hb_i][:, :], in_=hbd[r0:r0 + CH, :]),
                            reads=[("hb", r0 // CH)], writes=[("hacc", hb_i)])
                        if c == 0:
                            dump("hbw0", hacc[hb_i][:, :], D, [("hacc", hb_i)])
                        mlstm_chunk(0, c, hb_i, add_bwd=True)
                        if c == 0:
                            dump("hsum0", hacc[hb_i][:, :], D, [("hacc", hb_i)])
                            dump("eaflo0", eaflo[0][:, :], 8, [("eaflo", 0)])
                            dump("decb0", decb[0][:, :], 4, [("decb", 0)])
                        if c == 1:
                            dump("hsum1", hacc[hb_i][:, :], D, [("hacc", hb_i)])
                            dump("eaflo1", eaflo[1][:, :], 8, [("eaflo", 1)])
                            dump("decb1", decb[1][:, :], 4, [("decb", 1)])
                        for hd in range(NH):
                            hsl = slice(hd * HD, (hd + 1) * HD)
                            op("act", lambda e, hsl=hsl, hd=hd, hb_i=hb_i: e.activation(
                                out=junk[:, :], in_=hacc[hb_i][:, hsl], func=AF.Square,
                                accum_out=ssq[:, hd:hd + 1]),
                               reads=[("hacc", hb_i)], writes=["junk", ("ssq", hd)])
                        op("dve", lambda e: e.tensor_scalar(
                            out=ssq[:, 4:8], in0=ssq[:, 0:4], scalar1=1.0 / HD, scalar2=EPS, op0=ALU.mult,
                            op1=ALU.add), reads=[("ssq", h_) for h_ in range(NH)], writes=["ssq2"])
                        op("act", lambda e: e.activation(out=ssq[:, 4:8], in_=ssq[:, 4:8], func=AF.Sqrt),
                           reads=["ssq2"], writes=["ssq2b"])
                        op("dve", lambda e: e.reciprocal(out=ssq[:, 4:8], in_=ssq[:, 4:8]),
                           reads=["ssq2b"], writes=["ssq3"])
                        for hd in range(NH):
                            hsl = slice(hd * HD, (hd + 1) * HD)
                            if hd % 2:
                                op("act", lambda e, hsl=hsl, hd=hd, hb_i=hb_i: e.activation(
                                    out=hacc[hb_i][:, hsl], in_=hacc[hb_i][:, hsl], func=AF.Identity,
                                    scale=ssq[:, 4 + hd:5 + hd]),
                                   reads=[("hacc", hb_i), "ssq3"], writes=[("hacc", hb_i)])
                            else:
                                op("dve", lambda e, hsl=hsl, hd=hd, hb_i=hb_i: e.tensor_scalar(
                                    out=hacc[hb_i][:, hsl], in0=hacc[hb_i][:, hsl], scalar1=ssq[:, 4 + hd:5 + hd],
                                    scalar2=None, op0=ALU.mult),
                                   reads=[("hacc", hb_i), "ssq3"], writes=[("hacc", hb_i)])
                        for half in range(2):
                            bank = next_bank()
                            for jj in range(4):
                                j = half * 4 + jj
                                op("pe", lambda e, jj=jj, j=j, bank=bank, hb_i=hb_i: e.transpose(
                                    ps[bank][:, jj * 128:(jj + 1) * 128], hacc[hb_i][:, j * 128:(j + 1) * 128], ident),
                                   reads=[("hacc", hb_i), "cst"], writes=[("P", bank, 0)])
                            op("dve", lambda e, half=half, bank=bank, c=c: e.tensor_tensor(
                                out=ymT3[:, half * 4:half * 4 + 4, c * CH:(c + 1) * CH],
                                in0=ps[bank][:, :].rearrange("p (j t) -> p j t", t=128),
                                in1=GT3[:, half * 4:half * 4 + 4, c * CH:(c + 1) * CH], op=ALU.mult),
                               reads=[("P", bank, 0)] + [("GT", half * 4 + jj) for jj in range(4)],
                               writes=[("ymT", half * 4 + jj) for jj in range(4)])
                        if idx + 1 < len(tilesB):
                            if c == 0:
                                prep1(tilesB[idx + 1], idx + 1)
                            if c == 1:
                                prep2(tilesB[idx + 1], True)
                    dump("ymT", ymT[:, :], KC * T, [("ymT", k) for k in range(KC)], bf=True)
                    cwo = l * PP_L + PP_CW
                    cbo = l * PP_L + PP_CB
                    for j in range(8):
                        i2 = j % 2

                        def halo_mm(slot, col):
                            for k in range(KC):
                                op("pe", lambda e, k=k, slot=slot, col=col: e.matmul(
                                    ps[6][:, col:col + 2], lhsT=wr3[slot][:, k, :], rhs=H["v"][:, k, 0:XW:XW - 1],
                                    start=(k == 0), stop=(k == KC - 1)),
                                   reads=[("w", slot)] + hT_all, writes=[("P", 6, col)])
                        b_cc = proj_block(l, B_CC + j, rhs_h, hT_all, extra=lambda slot: halo_mm(slot, 304))
                        op("act", lambda e, b_cc=b_cc, i2=i2: e.activation(out=ubuf[i2][:, 1:T + 1], in_=ps[b_cc][:, :],
                                                                             func=AF.Copy),
                           reads=[("P", b_cc, 0)], writes=[("ubuf", i2)])
                        op("act", lambda e: e.activation(out=cch[:, :], in_=ps[6][:, 304:306], func=AF.Copy),
                           reads=[("P", 6, 304)], writes=["cch"])
                        b_cx = proj_block(l, B_CX + j, rhs_h, hT_all, extra=lambda slot: halo_mm(slot, 308))
                        op("dve", lambda e, b_cx=b_cx, i2=i2: e.tensor_tensor(
                            out=ubuf[i2][:, 1:T + 1], in0=ps[b_cx][:, :], in1=ubuf[i2][:, 1:T + 1], op=ALU.mult),
                           reads=[("P", b_cx, 0), ("ubuf", i2)], writes=[("ubuf", i2)])
                        op("dve", lambda e, i2=i2: e.tensor_tensor(
                            out=ubuf[i2][:, 0:XW:XW - 1], in0=ps[6][:, 308:310], in1=cch[:, :], op=ALU.mult),
                           reads=[("P", 6, 308), "cch", ("ubuf", i2)], writes=[("ubuf", i2)])
                        b_cz = proj_block(l, B_CZ + j, rhs_h, hT_all)
                        op("act", lambda e, b_cz=b_cz, i2=i2: e.activation(out=csz[i2][:, :], in_=ps[b_cz][:, :],
                                                                             func=AF.Silu),
                           reads=[("P", b_cz, 0)], writes=[("csz", i2)])
                        b_cb = proj_block(l, B_CB + j, rhs_h, hT_all)
                        dump("u0", ubuf[i2][:, :], XW, [("ubuf", i2)])
                        op("act", lambda e, i2=i2, j=j: e.activation(
                            out=ct1[i2][:, :], in_=ubuf[i2][:, 1:T + 1], func=AF.Identity,
                            scale=pp[:, cwo + 8 + j:cwo + 9 + j], bias=pp[:, cbo + j:cbo + j + 1]),
                           reads=[("ubuf", i2), "pp"], writes=[("ct1", i2)])
                        dump("c1", ct1[i2][:, :], T, [("ct1", i2)])
                        op("dve", lambda e, i2=i2, j=j: e.scalar_tensor_tensor(
                            out=ct1[i2][:, :], in0=ubuf[i2][:, 0:T], scalar=pp[:, cwo + j:cwo + j + 1],
                            in1=ct1[i2][:, :], op0=ALU.mult, op1=ALU.add),
                           reads=[("ubuf", i2), "pp", ("ct1", i2)], writes=[("ct1", i2)])
                        op("dve", lambda e, i2=i2, j=j: e.scalar_tensor_tensor(
                            out=ct1[i2][:, :], in0=ubuf[i2][:, 2:T + 2], scalar=pp[:, cwo + 16 + j:cwo + 17 + j],
                            in1=ct1[i2][:, :], op0=ALU.mult, op1=ALU.add),
                           reads=[("ubuf", i2), "pp", ("ct1", i2)], writes=[("ct1", i2)])
                        dump("c3", ct1[i2][:, :], T, [("ct1", i2)])
                        op("dve", lambda e, b_cb=b_cb, i2=i2: e.tensor_tensor(
                            out=ct1[i2][:, :], in0=ps[b_cb][:, :], in1=ct1[i2][:, :], op=ALU.mult),
                           reads=[("P", b_cb, 0), ("ct1", i2)], writes=[("ct1", i2)])
                        dump("c4", ct1[i2][:, :], T, [("ct1", i2)])
                        dump("csz", csz[i2][:, :], T, [("csz", i2)])
                        op("dve", lambda e, i2=i2, j=j: e.tensor_tensor(
                            out=ycT3[:, j, :], in0=ct1[i2][:, :], in1=csz[i2][:, :], op=ALU.mult),
                           reads=[("ct1", i2), ("csz", i2)], writes=[("vT", j)])
                    dump("ycT", ycT[:, :], KC * T, [("vT", k) for k in range(KC)], bf=True)
                    yc_all = [("vT", k) for k in range(KC)]
                    ym_all = [("ymT", k) for k in range(KC)]
                    for j in range(8):
                        i2 = j % 2
                        b_gc = proj_block(l, B_GC + j, rhs_h, hT_all)
                        op("act", lambda e, b_gc=b_gc, i2=i2: e.activation(out=sg1[i2][:, :], in_=ps[b_gc][:, :],
                                                                             func=AF.Sigmoid),
                           reads=[("P", b_gc, 0)], writes=[("sg1", i2)])
                        b_gm = proj_block(l, B_GM + j, rhs_h, hT_all)
                        op("act", lambda e, b_gm=b_gm, i2=i2: e.activation(out=sg2[i2][:, :], in_=ps[b_gm][:, :],
                                                                             func=AF.Sigmoid),
                           reads=[("P", b_gm, 0)], writes=[("sg2", i2)])
                        b_pc = proj_block(l, B_PC + j, lambda k: ycT3[:, k, :], yc_all)
                        op("dve", lambda e, b_pc=b_pc, i2=i2: e.tensor_tensor(
                            out=sg1[i2][:, :], in0=ps[b_pc][:, :], in1=sg1[i2][:, :], op=ALU.mult),
                           reads=[("P", b_pc, 0), ("sg1", i2)], writes=[("sg1", i2)])
                        b_pm = proj_block(l, B_PM + j, lambda k: ymT3[:, k, :], ym_all)
                        op("dve", lambda e, b_pm=b_pm, i2=i2: e.tensor_tensor(
                            out=sg2[i2][:, :], in0=ps[b_pm][:, :], in1=sg2[i2][:, :], op=ALU.mult),
                           reads=[("P", b_pm, 0), ("sg2", i2)], writes=[("sg2", i2)])
                        op("dve", lambda e, i2=i2, j=j: e.tensor_tensor(
                            out=mgT3[:, j, :], in0=sg1[i2][:, :], in1=sg2[i2][:, :], op=ALU.add),
                           reads=[("sg1", i2), ("sg2", i2)], writes=[("qT", j)])
                    dump("mgT", mgT[:, :], KC * T, [("qT", k) for k in range(KC)], bf=True)
                    mg_all = [("qT", k) for k in range(KC)]
                    o0 = (l * S + si) * 24
                    for j in range(8):
                        b_o = proj_block(l, B_WO + j, lambda k: mgT3[:, k, :], mg_all)
                        i2 = j % 2
                        op("dve", lambda e, b_o=b_o, i2=i2, j=j: e.scalar_tensor_tensor(
                            out=ct1[i2][:, :], in0=ps[b_o][:, :], scalar=modsb[:, o0 + 16 + j:o0 + 17 + j],
                            in1=xt3[bi][:, j, 1:T + 1], op0=ALU.mult, op1=ALU.add),
                           reads=[("P", b_o, 0), "modsb", ("xt", bi)], writes=[("ct1", i2)])
                        dma("sp", ("ct1", i2), lambda e, i2=i2, j=j, t0=t0: e.dma_start(
                            out=xTd[nxt][j * 128:(j + 1) * 128, t0:t0 + T], in_=ct1[i2][:, :]),
                            reads=[("ct1", i2)], writes=[("xT", nxt, t0 // T, j)])
            cur = nxt

        ep = []
        for ti in range(ntile_all):
            for c in range(NCH):
                ep.append((ti, c))
        ep_bi = {}

        def ep_s1(n):
            nonlocal pi
            ti, c = ep[n]
            t0 = ti * T
            if c == 0:
                bi = pi % 2
                pi += 1
                ep_bi[ti] = bi
                dma("sp", ("xt", bi), lambda e: e.dma_start(
                    out=xt3[bi][:, :, 0:T], in_=xTd[cur][:, t0:t0 + T].rearrange("(k p) t -> p k t", p=128)),
                    reads=[("xT", cur, ti, k_) for k_ in range(KC)], writes=[("xt", bi)])
            bi = ep_bi[ti]
            hb_i = n % 3
            for half in range(2):
                bank = next_bank()
                for jj in range(4):
                    k = half * 4 + jj
                    op("pe", lambda e, jj=jj, k=k: e.transpose(
                        ps[bank][:, jj * 128:(jj + 1) * 128], xt3[bi][:, k, c * CH:(c + 1) * CH], ident),
                       reads=[("xt", bi), "cst"], writes=[("P", bank, 0)])
                if half == 0:
                    op("dve", lambda e: e.tensor_copy(out=hacc[hb_i][:, 0:512], in_=ps[bank][:, :]),
                       reads=[("P", bank, 0)], writes=[("hacc", hb_i)])
                else:
                    op("act", lambda e: e.activation(out=hacc[hb_i][:, 512:1024], in_=ps[bank][:, :], func=AF.Copy),
                       reads=[("P", bank, 0)], writes=[("hacc", hb_i)])

        def ep_s2(n):
            ti, c = ep[n]
            t0 = ti * T
            hb_i = n % 3
            o = 8 * (n % 2)
            for q_ in range(4):
                op("act", lambda e, q_=q_: e.activation(
                    out=junk[:, :], in_=hacc[hb_i][:, q_ * HD:(q_ + 1) * HD], func=AF.Square,
                    accum_out=ssqe[:, o + q_:o + q_ + 1]), reads=[("hacc", hb_i)], writes=["junk", ("ssqe", o, q_)])
            op("dve", lambda e: e.tensor_reduce(out=ssqe[:, o + 4:o + 5], in_=ssqe[:, o:o + 4], axis=AX.X, op=ALU.add),
               reads=[("ssqe", o, q_) for q_ in range(4)], writes=[("ssqe", o, 4)])
            op("dve", lambda e: e.tensor_scalar(out=ssqe[:, o + 5:o + 6], in0=ssqe[:, o + 4:o + 5], scalar1=1.0 / D,
                                                scalar2=EPS, op0=ALU.mult, op1=ALU.add),
               reads=[("ssqe", o, 4)], writes=[("ssqe", o, 5)])
            op("act", lambda e: e.activation(out=ssqe[:, o + 6:o + 7], in_=ssqe[:, o + 5:o + 6], func=AF.Sqrt),
               reads=[("ssqe", o, 5)], writes=[("ssqe", o, 6)])
            op("dve", lambda e: e.reciprocal(out=ssqe[:, o + 7:o + 8], in_=ssqe[:, o + 6:o + 7]),
               reads=[("ssqe", o, 6)], writes=[("ssqe", o, 7)])
            op("dve", lambda e: e.scalar_tensor_tensor(
                out=hacc[hb_i][:, :], in0=hacc[hb_i][:, :], scalar=ssqe[:, o + 7:o + 8], in1=fg[:, :],
                op0=ALU.mult, op1=ALU.mult), reads=[("hacc", hb_i), ("ssqe", o, 7), "fg"], writes=[("hacc", hb_i)])
            r0 = t0 + c * CH
            dma("sp", ("hacc", hb_i), lambda e: e.dma_start(
                out=yout[r0:r0 + CH, :], in_=hacc[hb_i][:, :]), reads=[("hacc", hb_i)], writes=[("y", r0)])

        ep_s1(0)
        for n in range(len(ep)):
            if n + 1 < len(ep):
                ep_s1(n + 1)
            ep_s2(n)

        keys = list(S_.count.keys())
        sems = {}
        for i, k in enumerate(keys):
            sems[k] = es.enter_context(nc.semaphore(f"s{i}"))
        final_waits = [(k, S_.count[k]) for k in keys]
        eng_objs = {"pe": "tensor", "act": "scalar", "dve": "vector", "pool": "gpsimd", "sp": "sync"}
        with nc.Block() as block:
            def make(engname):
                def body(e):
                    for waits, fn, inckey, amt in S_.ops[engname]:
                        for k, v in waits:
                            e.wait_ge(sems[k], v)
                        fn(e).then_inc(sems[inckey], amt)
                    if engname == "sp":
                        for k, v in final_waits:
                            e.wait_ge(sems[k], v)
                return body
            for en, attr in eng_objs.items():
                getattr(block, attr)(make(en))
    return nc


def _host_layout(inputs, depth, core_seqs):
    L = depth
    if L == 0:
        shared0 = {"winb": np.zeros((1, NBLK, 128, KC * 128), np.float32), "wgin": np.zeros((1, 128, KC * 16), np.float32),
                   "wada": np.zeros((1, 128, KC, 3 * D), np.float32), "pp": np.zeros((128, PP_L), np.float32),
                   "bg": np.zeros((128, 16), np.float32)}
    w_in = np.asarray(inputs["w_in"], np.float32)[:max(L, 1)]
    colblocks = list(range(0, 72)) + [None] * 0
    blocks = []
    for l in range(L):
        lay = []
        wi = w_in[l]
        cols = np.concatenate([wi[:, 0:9216], wi[:, 9232:11280]], axis=1)
        mats = [cols, np.asarray(inputs["w_proj_conv"][l]), np.asarray(inputs["w_proj_mlstm"][l]),
                np.asarray(inputs["w_out"][l])]
        allc = np.concatenate(mats, axis=1)
        blk = allc.reshape(KC, 128, NBLK, 128).transpose(2, 1, 0, 3)
        blocks.append(blk.reshape(NBLK, 128, KC * 128))
    winb = np.ascontiguousarray(np.stack(blocks, 0), dtype=np.float32) if L else None
    wgin = np.ascontiguousarray(
        w_in[:, :, 9216:9232].reshape(L, KC, 128, 16).transpose(0, 2, 1, 3).reshape(L, 128, KC * 16))
    wada = np.ascontiguousarray(
        np.asarray(inputs["w_ada"], np.float32)[:L].reshape(L, KC, 128, 3 * D).transpose(0, 2, 1, 3))

    def fT(v, n):
        return np.asarray(v, np.float32).reshape(n, 128).T

    pp = np.zeros((128, L * PP_L), np.float32)
    for l in range(L):
        o = l * PP_L
        pp[:, o + PP_BADA:o + PP_BADA + 24] = fT(inputs["b_ada"][l], 24)
        pp[:, o + PP_NG:o + PP_NG + 8] = fT(inputs["norm_g"][l], 8)
        for tap in range(3):
            pp[:, o + PP_CW + tap * 8:o + PP_CW + tap * 8 + 8] = fT(inputs["conv_w"][l][tap], 8)
        pp[:, o + PP_CB:o + PP_CB + 8] = fT(inputs["conv_b"][l], 8)
        pp[:, o + PP_MG:o + PP_MG + 8] = fT(inputs["mh_norm_g"][l], 8)
    bg = np.ascontiguousarray(np.broadcast_to(
        np.asarray(inputs["b_gates"], np.float32)[:L].reshape(1, L * 16), (128, L * 16)))
    fg = np.ascontiguousarray(np.broadcast_to(np.asarray(inputs["final_norm_g"], np.float32)[None, :], (128, D)))
    ii = np.arange(128)
    cst = np.concatenate([np.eye(128, dtype=np.float32),
                          (ii[:, None] <= ii[None, :]).astype(np.float32),
                          (ii[:, None] >= ii[None, :]).astype(np.float32)], axis=1)
    shared = {"winb": winb, "wgin": wgin, "wada": wada, "pp": pp, "bg": bg, "fg": fg, "cst": cst}
    if L == 0:
        shared.update(shared0)
    in_maps = []
    for seqs in core_seqs:
        xs, cs = [], []
        for which, b in seqs:
            xs.append(np.asarray(inputs["x_" + which][b], np.float32))
            cs.append(np.asarray(inputs["c_" + which][b], np.float32))
        xin = np.ascontiguousarray(np.concatenate(xs, axis=0))
        c = np.stack(cs, 0)
        cT = np.ascontiguousarray(c.reshape(len(cs), KC, 128).transpose(2, 1, 0).reshape(128, KC * len(cs)))
        m = dict(shared)
        m["xin"] = xin
        m["cT"] = cT
        in_maps.append(m)
    return in_maps


def _run(inputs, depth, n_cores=8, trace=False):
    xp = np.asarray(inputs["x_prompt"])
    xs = np.asarray(inputs["x_sample"])
    nbp, lp = xp.shape[0], xp.shape[1]
    nbs, ls = xs.shape[0], xs.shape[1]
    pp_core, ps_core = nbp // n_cores, nbs // n_cores
    core_seqs = []
    for c in range(n_cores):
        core_seqs.append([("prompt", c * pp_core + i) for i in range(pp_core)] +
                         [("sample", c * ps_core + i) for i in range(ps_core)])
    seq_lens = [lp] * pp_core + [ls] * ps_core
    nc = _build(seq_lens, depth)
    in_maps = _host_layout(inputs, depth, core_seqs)
    res = run_bass_kernel_spmd(nc, in_maps, core_ids=list(range(n_cores)), **({"trace": True} if trace else {}))
    yp = np.zeros(xp.shape, np.float32)
    ys = np.zeros(xs.shape, np.float32)
    for c in range(n_cores):
        y = res.results[c]["y"]
        o = 0
        for which, b in core_seqs[c]:
            ln = lp if which == "prompt" else ls
            (yp if which == "prompt" else ys)[b] = y[o:o + ln]
            o += ln
    return (yp, ys), res


def kernel(**inputs):
    out, _ = _run(inputs, depth=4)
    return out
```

```python
import contextlib
import numpy as np
import concourse.bass as bass
import concourse.mybir as mybir
from concourse.bass_utils import run_bass_kernel_spmd

F32 = mybir.dt.float32
BF16 = mybir.dt.bfloat16
AF = mybir.ActivationFunctionType
ALU = mybir.AluOpType
AX = mybir.AxisListType

D = 1024
KC = 8
NH = 4
HD = 256
CH = 128
T = 512
NCH = T // CH
EPS = 1e-6
D_IN = 11280
NBLK = 112
B_CB, B_CC, B_CX, B_CZ, B_Q, B_K, B_V, B_O, B_MZ, B_GC, B_GM, B_PC, B_PM, B_WO = (
    0, 8, 16, 24, 32, 40, 48, 56, 64, 72, 80, 88, 96, 104)
NSLOT = 10
PP_BADA, PP_NG, PP_CW, PP_CB, PP_MG, PP_L = 0, 24, 32, 56, 64, 72


class _Rec:
    def __init__(self):
        self.call = None

    def __getattr__(self, name):
        def f(*a, **kw):
            self.call = (name, a, kw)
            return self
        return f


def _record(fn):
    r = _Rec()
    fn(r)
    name, a, kw = r.call
    return lambda e: getattr(e, name)(*a, **kw)


class Sched:
    ENGS = ("pe", "act", "dve", "pool", "sp")

    def __init__(self):
        self.ops = {e: [] for e in self.ENGS}
        self.count = {}
        self.waited = {e: {} for e in self.ENGS}
        self.last_write = {}
        self.readers = {}
        self.bank_last = {}

    def _deps(self, reads, writes, mykey):
        deps = {}

        def add(tok):
            if tok is None:
                return
            k, v = tok
            if k == mykey:
                return
            if deps.get(k, 0) < v:
                deps[k] = v

        for r in reads:
            add(self.last_write.get(r))
        for w in writes:
            add(self.last_write.get(w))
            for k, v in self.readers.get(w, {}).items():
                add((k, v))
        return deps

    def _commit(self, reads, writes, tok):
        k, v = tok
        for r in reads:
            d = self.readers.setdefault(r, {})
            if d.get(k, 0) < v:
                d[k] = v
        for w in writes:
            self.last_write[w] = tok
            self.readers[w] = {}

    def _push(self, eng, deps, fn, inckey, amt):
        waits = []
        wd = self.waited[eng]
        for k, v in deps.items():
            if wd.get(k, 0) < v:
                wd[k] = v
                waits.append((k, v))
        self.ops[eng].append((waits, _record(fn), inckey, amt))

    def op(self, eng, fn, reads=(), writes=()):
        key = eng
        deps = {}

        def add(tok):
            if tok is None:
                return
            k, v = tok
            if deps.get(k, 0) < v:
                deps[k] = v

        for r in reads:
            add(self.last_write.get(r))
        for w in writes:
            lw = self.last_write.get(w)
            if lw is not None and not (lw[0] == key and eng == "pe"):
                add(lw)
            for k, v in self.readers.get(w, {}).items():
                add((k, v))
        if eng == "pe":
            deps.pop("pe", None)
        n = self.count.get(key, 0) + 1
        self.count[key] = n
        banks = {r[1] for r in list(reads) + list(writes) if isinstance(r, tuple) and r and r[0] == "P"}
        for b in banks:
            bl = self.bank_last.setdefault(b, {})
            for k, v in bl.items():
                if k != key:
                    add((k, v))
            bl[key] = n
        self._push(eng, deps, fn, key, 1)
        self._commit(reads, writes, (key, n))

    def dma(self, q, group, fn, reads=(), writes=()):
        key = ("g", group)
        deps = self._deps(reads, writes, key)
        n = self.count.get(key, 0) + 16
        self.count[key] = n
        self._push(q, deps, fn, key, 16)
        self._commit(reads, writes, (key, n))


DBG = {"on": False, "map": {}}


def _build(seq_lens, depth, debug_stage=99):
    nc = bass.Bass("TRN2", target_bir_lowering=False)
    TC = sum(seq_lens)
    S = len(seq_lens)
    L = depth
    LW = max(L, 1)

    def dram(name, shape, dt, kind):
        return nc.dram_tensor(name, list(shape), dt, kind=kind).ap()

    xin = dram("xin", [TC, D], F32, "ExternalInput")
    winb = dram("winb", [LW, NBLK, 128, KC * 128], F32, "ExternalInput")
    wgin = dram("wgin", [LW, 128, KC * 16], F32, "ExternalInput")
    wada = dram("wada", [LW, 128, KC, 3 * D], F32, "ExternalInput")
    cTin = dram("cT", [128, KC * S], F32, "ExternalInput")
    ppin = dram("pp", [128, LW * PP_L], F32, "ExternalInput")
    bgin = dram("bg", [128, LW * 16], F32, "ExternalInput")
    fgin = dram("fg", [128, D], F32, "ExternalInput")
    cstin = dram("cst", [128, 3 * 128], F32, "ExternalInput")
    yout = dram("y", [TC, D], F32, "ExternalOutput")
    xTd = [dram("xTa", [D, TC], F32, "Internal"), dram("xTb", [D, TC], F32, "Internal")]
    hbd = dram("hb", [TC, D], F32, "Internal")
    wbf = dram("wbf", [LW, NBLK, 128, KC * 128], BF16, "Internal")

    if DBG["on"]:
        dbgf = dram("dbgf", [128, 32768], F32, "ExternalOutput")
        dbgb = dram("dbgb", [128, 65536], BF16, "ExternalOutput")
        DBG["map"] = {}
        DBG["off"] = {"f": 0, "b": 0}

    def dump(name, ap, n, reads, bf=False):
        if not DBG["on"] or name in DBG["map"]:
            return
        kind = "b" if bf else "f"
        o = DBG["off"][kind]
        DBG["off"][kind] = o + n
        DBG["map"][name] = (kind, o, n)
        dst = (dbgb if bf else dbgf)[:, o:o + n]
        S_.dma("sp", ("dbg", name), lambda e: e.dma_start(out=dst, in_=ap), reads=reads, writes=[("dbg", name)])

    es = contextlib.ExitStack()
    with es:
        def sb(name, shape, dt):
            return es.enter_context(nc.sbuf_tensor(name, list(shape), dt))

        def psb(name):
            return es.enter_context(nc.psum_tensor(name, [128, 512], F32))

        XW = T + 2
        xt = [sb(f"xt{i}", [128, KC * XW], F32) for i in range(2)]
        sq = sb("sq", [128, KC * XW], BF16)
        rstd = sb("rstd", [128, XW], F32)
        tmpA = [sb(f"tmpA{i}", [128, XW], F32) for i in range(2)]
        hT = sb("hT", [128, KC * XW], BF16)
        hTb = sb("hTb", [128, KC * XW], BF16)
        qT = sb("qT", [128, KC * T], BF16)
        kT = sb("kT", [128, KC * T], BF16)
        vT = sb("vT", [128, KC * T], BF16)
        ymT = sb("ymT", [128, KC * T], BF16)
        GT = sb("GT", [128, KC * T], F32)
        wring = [sb(f"wr{i}", [128, KC * 128], BF16) for i in range(NSLOT)]
        hacc = [sb(f"hacc{i}", [128, D], F32) for i in range(3)]
        ssqe = sb("ssqe", [128, 16], F32)
        Cst = sb("Cst", [128, NH * 2 * 257], F32)
        Cbf = sb("Cbf", [128, NH * 2 * 257], BF16)
        ktok = [sb(f"ktok{i}", [128, D], BF16) for i in range(NCH)]
        vaug = [sb(f"vaug{i}", [128, NH * 257], BF16) for i in range(NCH)]
        kp = [sb(f"kp{i}", [128, HD], BF16) for i in range(2)]
        SD = [sb(f"SD{i}", [128, CH], BF16) for i in range(2)]
        ubuf = [sb(f"ubuf{i}", [128, XW], F32) for i in range(2)]
        cch = sb("cch", [128, 2], F32)
        ct1 = [sb(f"ct1_{i}", [128, T], F32) for i in range(2)]
        csz = [sb(f"csz{i}", [128, T], F32) for i in range(2)]
        sg1 = [sb(f"sg1_{i}", [128, T], F32) for i in range(2)]
        sg2 = [sb(f"sg2_{i}", [128, T], F32) for i in range(2)]
        cst = sb("cst_sb", [128, 3 * 128], F32)
        identb = sb("identb", [128, 128], BF16)
        maskb = sb("maskb", [128, 2 * 128], BF16)
        onesb = sb("onesb", [128, 128], BF16)
        ones4 = sb("ones4", [4, 128], F32)
        epsb = sb("epsb", [128, 1], F32)
        pp = sb("pp_sb", [128, LW * PP_L], F32)
        bg = sb("bg_sb", [128, LW * 16], F32)
        fg = sb("fg_sb", [128, D], F32)
        cT = sb("cT_sb", [128, KC * S], F32)
        wg = sb("wg_sb", [128, LW * KC * 16], BF16)
        modsb = sb("modsb", [128, LW * S * 24], F32)
        gmod = sb("gmod", [128, LW * S * 8], F32)
        gsb = sb("gsb", [128, 8], F32)
        esp = sb("esp", [128, 8], F32)
        er = sb("er", [4, 256], F32)
        mst = sb("mst", [4, 8], F32)
        ddiag = sb("ddiag", [4, 4], F32)
        eaflo = [sb(f"eaflo{i}", [128, 8], F32) for i in range(NCH)]
        decb = [sb(f"decb{i}", [128, 4], F32) for i in range(NCH)]
        den = sb("den", [128, 8], F32)
        ssq = sb("ssq", [128, 8], F32)
        junk = sb("junk", [128, HD], F32)
        ps = [psb(f"ps{i}") for i in range(8)]
        print("SBUF bytes/partition remaining:", nc.sbuf_bytes_remaining)

        S_ = Sched()
        op, dma = S_.op, S_.dma

        def v3(t_, n):
            return t_[:, :].rearrange("p (k n) -> p k n", n=n)

        xt3 = [v3(x_, XW) for x_ in xt]
        sq3 = v3(sq, XW)
        hTv = [v3(hT, XW), v3(hTb, XW)]
        H = {"v": hTv[0]}
        qT3, kT3, vT3, ymT3, GT3 = (v3(t_, T) for t_ in (qT, kT, vT, ymT, GT))
        ycT3, mgT3 = vT3, qT3
        ycT, mgT = vT, qT
        wr3 = [v3(w_, 128) for w_ in wring]
        ident = cst[:, 0:128]
        Umat = [cst[:, 128:256], cst[:, 256:384]]

        def psbf(bank, lo, n):
            return ps[bank][:, :].bitcast(BF16)[:, lo:lo + n]

        dma("sp", "c0", lambda e: e.dma_start(out=cst[:, :], in_=cstin[:, :]), writes=["cst"])
        dma("sp", "c1", lambda e: e.dma_start(out=pp[:, :], in_=ppin[:, :]), writes=["pp"])
        dma("sp", "c2", lambda e: e.dma_start(out=bg[:, :], in_=bgin[:, :]), writes=["bg"])
        dma("sp", "c3", lambda e: e.dma_start(out=fg[:, :], in_=fgin[:, :]), writes=["fg"])
        dma("sp", "c4", lambda e: e.dma_start(out=cT[:, :], in_=cTin[:, :]), writes=["cT"])
        for l in range(L):
            dma("pool", ("c5", l), lambda e, l=l: e.dma_start(
                out=wg[:, l * KC * 16:(l + 1) * KC * 16], in_=wgin[l, :, :]), writes=[("wg", l)])
        op("dve", lambda e: e.tensor_copy(out=identb[:, :], in_=ident), reads=["cst"], writes=["identb"])
        op("dve", lambda e: e.tensor_copy(out=maskb[:, :], in_=cst[:, 128:384]), reads=["cst"], writes=["maskb"])
        op("dve", lambda e: e.memset(onesb[:, :], 1.0 / 1024.0), writes=["onesb"])
        op("dve", lambda e: e.memset(ones4[:, :], 1.0), writes=["ones4"])
        for c_ in range(NCH):
            op("dve", lambda e, c_=c_: e.memset(vaug[c_][:, :], 1.0), writes=[("vaug", c_)])
        op("dve", lambda e: e.memset(epsb[:, :], EPS), writes=["epsb"])

        for l in range(L):
            order = [32, 40, 48] + [b0 for b0 in range(0, NBLK, 8) if b0 not in (32, 40, 48)]
            for b0 in order:
                g = 0 if 32 <= b0 < 56 else 1
                dma("pool", ("wc", l, g), lambda e, l=l, b0=b0: e.dma_start(
                    out=wbf[l, b0:b0 + 8, :, :], in_=winb[l, b0:b0 + 8, :, :]), writes=[("wbf", l, g)])

        wa3 = [x_[:, 0:KC * 512].rearrange("p (k n) -> p k n", n=512) for x_ in xt]
        pi = 0
        for l in range(L):
            for n6 in range(6):
                bi = pi % 2
                pi += 1
                dma("sp" if bi == 0 else "act", ("xt", bi), lambda e, l=l, n6=n6, bi=bi: e.dma_start(
                    out=wa3[bi], in_=wada[l, :, :, n6 * 512:(n6 + 1) * 512]), writes=[("xt", bi)])
                for m in range(4):
                    mb = n6 * 4 + m
                    for k in range(KC):
                        op("pe", lambda e, bi=bi, m=m, k=k: e.matmul(
                            ps[0][:, m * 8:m * 8 + S], lhsT=wa3[bi][:, k, m * 128:(m + 1) * 128],
                            rhs=cT[:, k * S:(k + 1) * S], start=(k == 0), stop=(k == KC - 1)),
                           reads=[("xt", bi), "cT"], writes=[("P", 0, 0)])
                    o0 = (l * S) * 24 + mb
                    op("dve", lambda e, m=m, o0=o0, l=l, mb=mb: e.tensor_scalar(
                        out=modsb[:, o0:o0 + (S - 1) * 24 + 1:24], in0=ps[0][:, m * 8:m * 8 + S],
                        scalar1=pp[:, l * PP_L + PP_BADA + mb:l * PP_L + PP_BADA + mb + 1], scalar2=None,
                        op0=ALU.add), reads=[("P", 0, 0), "pp"], writes=["modsb"])
            for s in range(S):
                o0 = (l * S + s) * 24
                g0 = (l * S + s) * 8
                op("dve", lambda e, o0=o0, g0=g0, l=l: e.scalar_tensor_tensor(
                    out=gmod[:, g0:g0 + 8], in0=modsb[:, o0 + 8:o0 + 16], scalar=1.0,
                    in1=pp[:, l * PP_L + PP_NG:l * PP_L + PP_NG + 8], op0=ALU.add, op1=ALU.mult),
                   reads=["modsb", "pp"], writes=["gmod"])

        ntile_all = TC // T
        for ti in range(ntile_all):
            t0 = ti * T
            bi = pi % 2
            pi += 1
            xv = xt[bi][:, 0:NCH * D].rearrange("p (c d) -> p c d", d=D)
            dma("sp", ("xt", bi), lambda e, xv=xv, t0=t0: e.dma_start(
                out=xv, in_=xin[t0:t0 + T, :].rearrange("(c p) d -> p c d", p=128)), writes=[("xt", bi)])
            ob = ubuf
            for k in range(KC):
                bank = k % 3
                for c in range(NCH):
                    op("pe", lambda e, bank=bank, c=c, k=k, xv=xv: e.transpose(
                        ps[bank][:, c * 128:(c + 1) * 128], xv[:, c, k * 128:(k + 1) * 128], ident),
                       reads=[("xt", bi), "cst"], writes=[("P", bank, 0)])
                oi = k % 2
                eng = "dve" if k % 2 == 0 else "act"
                if eng == "dve":
                    op("dve", lambda e, bank=bank, oi=oi: e.tensor_copy(out=ob[oi][:, 0:T], in_=ps[bank][:, :]),
                       reads=[("P", bank, 0)], writes=[("ubuf", oi)])
                else:
                    op("act", lambda e, bank=bank, oi=oi: e.activation(out=ob[oi][:, 0:T], in_=ps[bank][:, :],
                                                                         func=AF.Copy),
                       reads=[("P", bank, 0)], writes=[("ubuf", oi)])
                dma("sp", ("ubuf", oi), lambda e, oi=oi, k=k, t0=t0: e.dma_start(
                    out=xTd[0][k * 128:(k + 1) * 128, t0:t0 + T], in_=ob[oi][:, 0:T]),
                    reads=[("ubuf", oi)], writes=[("xT", 0, ti, k)])

        seqs = []
        o = 0
        for ln in seq_lens:
            seqs.append((o, ln))
            o += ln

        def tile_blocks(pass_b):
            if not pass_b:
                return list(range(B_Q, B_Q + 24))
            out = list(range(B_Q, B_Q + 24)) + list(range(B_O, B_O + 16))
            for j in range(8):
                out += [B_CC + j, B_CX + j, B_CZ + j, B_CB + j]
            for j in range(8):
                out += [B_GC + j, B_GM + j, B_PC + j, B_PM + j]
            out += list(range(B_WO, B_WO + 8))
            return out

        wseq = []
        for l in range(L):
            for pass_b in (False, True):
                for (s0, ln) in seqs:
                    for _ in range(ln // T):
                        wseq += [(l, b) for b in tile_blocks(pass_b)]
        wstate = {"ld": 0, "use": 0}

        def w_prefetch(upto):
            while wstate["ld"] < min(upto, len(wseq)):
                i = wstate["ld"]
                l, b = wseq[i]
                slot = i % NSLOT
                dma("sp", ("w", slot), lambda e, l=l, b=b, slot=slot: e.dma_start(
                    out=wring[slot][:, :], in_=wbf[l, b, :, :]), reads=[("wbf", l, 0 if 32 <= b < 56 else 1)], writes=[("w", slot)])
                wstate["ld"] += 1

        def w_use(l, b):
            i = wstate["use"]
            assert wseq[i] == (l, b), (i, wseq[i], l, b)
            w_prefetch(i + NSLOT)
            wstate["use"] += 1
            return i % NSLOT

        bigctr = {"n": 0}

        def next_bank():
            b = bigctr["n"] % 3
            bigctr["n"] += 1
            return b

        def proj_block(l, b, rhs_fn, rhs_reads, n=T, extra=None):
            slot = w_use(l, b)
            bank = next_bank()
            for k in range(KC):
                op("pe", lambda e, slot=slot, bank=bank, k=k: e.matmul(
                    ps[bank][:, 0:n], lhsT=wr3[slot][:, k, :], rhs=rhs_fn(k),
                    start=(k == 0), stop=(k == KC - 1)),
                   reads=[("w", slot)] + rhs_reads, writes=[("P", bank, 0)])
            if extra is not None:
                extra(slot)
            return bank

        def load_x(cur, bi, s0, ln, t0):
            first = (t0 == s0)
            last = (t0 + T == s0 + ln)
            lo = t0 - (0 if first else 1)
            hi = t0 + T + (0 if last else 1)
            c0 = 0 if not first else 1
            for k in range(KC):
                pass
            dma("sp", ("xt", bi), lambda e: e.dma_start(
                out=xt3[bi][:, :, c0:c0 + (hi - lo)],
                in_=xTd[cur][:, lo:hi].rearrange("(k p) t -> p k t", p=128)),
                reads=[("xT", cur, tt, k_) for k_ in range(KC) for tt in
                       ([t0 // T] + ([t0 // T - 1] if not first else []) + ([t0 // T + 1] if not last else []))],
                writes=[("xt", bi)])
            if first:
                op("pool", lambda e: e.memset(xt3[bi][:, :, 0:1], 0.0), writes=[("xt", bi)], reads=[("xt", bi)])
            if last:
                op("pool", lambda e: e.memset(xt3[bi][:, :, XW - 1:XW], 0.0), writes=[("xt", bi)],
                   reads=[("xt", bi)])
            return first, last

        def norm_a(bi):
            for k in range(KC):
                op("act", lambda e, k=k: e.activation(out=sq3[:, k, :], in_=xt3[bi][:, k, :], func=AF.Square),
                   reads=[("xt", bi)], writes=[("sq", k)])

        def norm_b(l, si, bi, first, last, halo, hi):
            hv = hTv[hi]
            bank = next_bank()
            for k in range(KC):
                op("pe", lambda e, k=k, bank=bank: e.matmul(
                    ps[bank][:, :], lhsT=onesb[:, :], rhs=sq3[:, k, 1:T + 1], start=(k == 0), stop=(k == KC - 1)),
                   reads=[("sq", k), "onesb"], writes=[("P", bank, 0)])
            op("act", lambda e, bank=bank: e.activation(out=rstd[:, 1:T + 1], in_=ps[bank][:, :], func=AF.Sqrt,
                                                        bias=epsb[:, 0:1], scale=1.0),
               reads=[("P", bank, 0), "epsb"], writes=["rstd"])
            op("dve", lambda e: e.reciprocal(out=rstd[:, 1:T + 1], in_=rstd[:, 1:T + 1]),
               reads=["rstd"], writes=["rstd"])
            if halo:
                bank2 = next_bank()
                for k in range(KC):
                    op("pe", lambda e, k=k: e.matmul(
                        ps[bank2][:, 0:2], lhsT=onesb[:, :], rhs=sq3[:, k, 0:XW:XW - 1],
                        start=(k == 0), stop=(k == KC - 1)),
                       reads=[("sq", k), "onesb"], writes=[("P", bank2, 0)])
                op("act", lambda e: e.activation(out=rstd[:, 0:XW:XW - 1], in_=ps[bank2][:, 0:2], func=AF.Sqrt,
                                                 bias=epsb[:, 0:1], scale=1.0),
                   reads=[("P", bank2, 0), "epsb"], writes=["rstd"])
                op("dve", lambda e: e.reciprocal(out=rstd[:, 0:XW:XW - 1], in_=rstd[:, 0:XW:XW - 1]),
                   reads=["rstd"], writes=["rstd"])
            c_lo, c_hi = (0, XW) if halo else (1, T + 1)
            g0 = (l * S + si) * 8
            o0 = (l * S + si) * 24
            for k in range(KC):
                ti_ = k % 2
                op("dve", lambda e, k=k, ti_=ti_: e.scalar_tensor_tensor(
                    out=tmpA[ti_][:, c_lo:c_hi], in0=xt3[bi][:, k, c_lo:c_hi], scalar=gmod[:, g0 + k:g0 + k + 1],
                    in1=rstd[:, c_lo:c_hi], op0=ALU.mult, op1=ALU.mult),
                   reads=[("xt", bi), "gmod", "rstd"], writes=[("tmpA", ti_)])
                op("act", lambda e, k=k, ti_=ti_: e.activation(
                    out=hv[:, k, c_lo:c_hi], in_=tmpA[ti_][:, c_lo:c_hi], func=AF.Identity,
                    bias=modsb[:, o0 + k:o0 + k + 1], scale=1.0),
                   reads=[("tmpA", ti_), "modsb"], writes=[("hT", hi, k)])
            hall = [("hT", hi, k) for k in range(KC)]
            if halo and first:
                op("pool", lambda e: e.memset(hv[:, :, 0:1], 0.0), reads=hall, writes=hall)
            if halo and last:
                op("pool", lambda e: e.memset(hv[:, :, XW - 1:XW], 0.0), reads=hall, writes=hall)

        hT_all = [("hT", 0, k) for k in range(KC)]

        def set_h(hi):
            H["v"] = hTv[hi]
            hT_all[:] = [("hT", hi, k) for k in range(KC)]

        def rhs_h(k):
            return H["v"][:, k, 1:T + 1]

        GS0 = 260

        def gates_gen(l, dr, c):
            cs = slice(1 + c * CH, 1 + (c + 1) * CH)
            gcol = GS0 + c * 24
            wgo = l * KC * 16
            for k in range(KC):
                op("pe", lambda e, k=k: e.matmul(
                    ps[7][:, gcol:gcol + 8], lhsT=H["v"][:, k, cs],
                    rhs=wg[:, wgo + k * 16 + dr * 8:wgo + k * 16 + dr * 8 + 8],
                    start=(k == 0), stop=(k == KC - 1)),
                   reads=hT_all + [("wg", l)], writes=[("P", 7, ("g", c))])
            op("dve", lambda e: e.tensor_tensor(
                out=gsb[:, :], in0=ps[7][:, gcol:gcol + 8], in1=bg[:, l * 16 + dr * 8:l * 16 + dr * 8 + 8],
                op=ALU.add), reads=[("P", 7, ("g", c)), "bg"], writes=["gsb"])
            op("act", lambda e: e.activation(out=esp[:, 0:4], in_=gsb[:, 4:8], func=AF.Exp, scale=-1.0),
               reads=["gsb"], writes=["esp0"])
            op("act", lambda e: e.activation(out=esp[:, 4:8], in_=esp[:, 0:4], func=AF.Ln, bias=1.0, scale=1.0),
               reads=["esp0"], writes=["esp1"])
            yield
            r1 = 0
            op("pe", lambda e: e.matmul(ps[7][0:4, r1:r1 + 128], lhsT=esp[:, 4:8], rhs=Umat[dr],
                                        start=True, stop=False),
               reads=["esp1", "cst"], writes=[("P", 7, "r1")])
            op("pe", lambda e: e.matmul(ps[7][0:4, r1:r1 + 128], lhsT=gsb[:, 0:4], rhs=ident,
                                        start=False, stop=True),
               reads=["gsb", "cst"], writes=[("P", 7, "r1")])
            op("pe", lambda e: e.matmul(ps[7][0:4, r1 + 128:r1 + 256], lhsT=esp[:, 4:8], rhs=Umat[dr],
                                        start=True, stop=True),
               reads=["esp1", "cst"], writes=[("P", 7, "r1")])
            lastc = r1 + 128 + (127 if dr == 0 else 0)
            op("dve", lambda e: e.tensor_reduce(out=mst[:, 1:2], in_=ps[7][0:4, r1:r1 + 128], axis=AX.X, op=ALU.max),
               reads=[("P", 7, "r1")], writes=["amax"])
            op("dve", lambda e: e.tensor_tensor(out=mst[:, 2:3], in0=mst[:, 1:2], in1=mst[:, 0:1], op=ALU.max),
               reads=["amax", "mstate"], writes=["R"])
            op("dve", lambda e: e.tensor_scalar(out=mst[:, 3:4], in0=mst[:, 2:3], scalar1=-1.0, scalar2=None,
                                                op0=ALU.mult), reads=["R"], writes=["nR"])
            op("dve", lambda e: e.tensor_tensor(out=mst[:, 4:5], in0=mst[:, 0:1], in1=mst[:, 3:4], op=ALU.add),
               reads=["mstate", "nR"], writes=["dlt"])
            op("dve", lambda e: e.scalar_tensor_tensor(
                out=mst[:, 0:1], in0=ps[7][0:4, lastc:lastc + 1], scalar=-1.0, in1=mst[:, 2:3],
                op0=ALU.mult, op1=ALU.add), reads=[("P", 7, "r1"), "R"], writes=["mstate"])
            op("dve", lambda e: e.tensor_scalar(out=ddiag[:, :], in0=cst[0:4, 0:4], scalar1=mst[:, 4:5],
                                                scalar2=None, op0=ALU.mult), reads=["dlt", "cst"], writes=["ddiag"])
            op("act", lambda e: e.activation(out=er[:, 0:128], in_=ps[7][0:4, r1:r1 + 128], func=AF.Exp,
                                             bias=mst[:, 3:4], scale=1.0), reads=[("P", 7, "r1"), "nR"], writes=["er0"])
            op("act", lambda e: e.activation(out=er[:, 128:256], in_=ps[7][0:4, r1 + 128:r1 + 256], func=AF.Exp,
                                             bias=mst[:, 3:4], scale=1.0), reads=[("P", 7, "r1"), "nR"], writes=["er1"])
            yield
            op("pe", lambda e: e.matmul(ps[7][:, gcol + 8:gcol + 12], lhsT=er[:, 0:128], rhs=cst[0:4, 0:4],
                                        start=True, stop=True), reads=["er0", "cst"], writes=[("P", 7, ("g2", c))])
            op("pe", lambda e: e.matmul(ps[7][:, gcol + 12:gcol + 16], lhsT=er[:, 128:256], rhs=cst[0:4, 0:4],
                                        start=True, stop=True), reads=["er1", "cst"], writes=[("P", 7, ("g2", c))])
            op("pe", lambda e: e.matmul(ps[7][:, gcol + 16:gcol + 20], lhsT=ones4[:, :], rhs=ddiag[:, :],
                                        start=True, stop=True), reads=["ones4", "ddiag"], writes=[("P", 7, ("g2", c))])
            op("dve", lambda e: e.tensor_copy(out=eaflo[c][:, :], in_=ps[7][:, gcol + 8:gcol + 16]),
               reads=[("P", 7, ("g2", c))], writes=[("eaflo", c)])
            op("act", lambda e: e.activation(out=decb[c][:, :], in_=ps[7][:, gcol + 16:gcol + 20], func=AF.Exp),
               reads=[("P", 7, ("g2", c))], writes=[("decb", c)])
            yield

        def phase2(l, dr, chunk_order, pass_b):
            def gsteps():
                for c in chunk_order:
                    yield from gates_gen(l, dr, c)
            gs = gsteps()

            def tick():
                try:
                    next(gs)
                except StopIteration:
                    pass

            for j in range(8):
                bank = proj_block(l, B_Q + j, rhs_h, hT_all)
                op("act", lambda e, j=j, bank=bank: e.activation(out=qT3[:, j, :], in_=ps[bank][:, :], func=AF.Copy),
                   reads=[("P", bank, 0)], writes=[("qT", j)])
                tick()
            for j in range(8):
                bank = proj_block(l, B_K + j, rhs_h, hT_all)
                op("dve", lambda e, j=j, bank=bank: e.tensor_scalar(
                    out=kT3[:, j, :], in0=ps[bank][:, :], scalar1=HD ** -0.5, scalar2=None, op0=ALU.mult),
                   reads=[("P", bank, 0)], writes=[("kT", j)])
                tick()
            kT_all = [("kT", k) for k in range(KC)]
            vT_all = [("vT", k) for k in range(KC)]
            for c in chunk_order:
                cs = slice(c * CH, (c + 1) * CH)
                bank = next_bank()
                for j in range(KC):
                    op("pe", lambda e, j=j, bank=bank: e.transpose(
                        psbf(bank, j * 128, 128), kT3[:, j, cs], identb[:, :]),
                       reads=kT_all + ["identb"], writes=[("P", bank, 0)])
                op("act", lambda e, bank=bank: e.activation(out=ktok[c][:, :], in_=psbf(bank, 0, 1024), func=AF.Copy),
                   reads=[("P", bank, 0)], writes=[("ktok", c)])
            for j in range(8):
                bank = proj_block(l, B_V + j, rhs_h, hT_all)
                if j % 2 == 0:
                    op("act", lambda e, j=j, bank=bank: e.activation(out=vT3[:, j, :], in_=ps[bank][:, :],
                                                                         func=AF.Copy),
                       reads=[("P", bank, 0)], writes=[("vT", j)])
                else:
                    op("dve", lambda e, j=j, bank=bank: e.tensor_copy(out=vT3[:, j, :], in_=ps[bank][:, :]),
                       reads=[("P", bank, 0)], writes=[("vT", j)])
                tick()
            for c in chunk_order:
                cs = slice(c * CH, (c + 1) * CH)
                bank = next_bank()
                for j in range(KC):
                    op("pe", lambda e, j=j, bank=bank: e.transpose(
                        psbf(bank, j * 128, 128), vT3[:, j, cs], identb[:, :]),
                       reads=vT_all + ["identb"], writes=[("P", bank, 0)])
                op("dve", lambda e, bank=bank: e.tensor_copy(
                    out=vaug[c][:, :].rearrange("p (h e) -> p h e", e=257)[:, :, 0:HD],
                    in_=psbf(bank, 0, 1024).rearrange("p (h e) -> p h e", e=HD)),
                   reads=[("P", bank, 0)], writes=[("vaug", c)])
            if pass_b:
                for j in range(8):
                    bo = proj_block(l, B_O + j, rhs_h, hT_all)
                    op("act", lambda e, bo=bo, j=j: e.activation(out=GT3[:, j, :], in_=ps[bo][:, :],
                                                                   func=AF.Sigmoid),
                       reads=[("P", bo, 0)], writes=[("GT", j)])
                    tick()
                for j in range(8):
                    bm = proj_block(l, B_MZ + j, rhs_h, hT_all)
                    i2 = j % 2
                    op("act", lambda e, bm=bm, i2=i2: e.activation(out=sg2[i2][:, :], in_=ps[bm][:, :],
                                                                     func=AF.Silu),
                       reads=[("P", bm, 0)], writes=[("sg2", i2)])
                    op("dve", lambda e, j=j, i2=i2: e.scalar_tensor_tensor(
                        out=GT3[:, j, :], in0=GT3[:, j, :],
                        scalar=pp[:, l * PP_L + PP_MG + j:l * PP_L + PP_MG + j + 1], in1=sg2[i2][:, :],
                        op0=ALU.mult, op1=ALU.mult),
                       reads=[("sg2", i2), ("GT", j), "pp"], writes=[("GT", j)])
                    tick()
            for _ in gs:
                pass

        cnt = {"sd": 0, "kp": 0, "np": 0}

        def mlstm_chunk(dr, c, hb_i, add_bwd):
            cs = slice(c * CH, (c + 1) * CH)
            for hd in range(NH):
                for j in range(2):
                    op("pe", lambda e, j=j, hd=hd: e.matmul(
                        ps[3][:, hd * 128:(hd + 1) * 128], lhsT=kT3[:, 2 * hd + j, cs], rhs=qT3[:, 2 * hd + j, cs],
                        start=(j == 0), stop=(j == 1)),
                       reads=[("kT", 2 * hd + j), ("qT", 2 * hd + j)], writes=[("P", 3, hd)])
            st = {}

            def pre(hd):
                di = cnt["sd"] % 2
                cnt["sd"] += 1
                ki = cnt["kp"] % 2
                cnt["kp"] += 1
                nb = 4 + (cnt["np"] % 2)
                cnt["np"] += 1
                st[hd] = (di, ki, nb)
                op("dve", lambda e: e.scalar_tensor_tensor(
                    out=SD[di][:, :], in0=ps[3][:, hd * 128:(hd + 1) * 128], scalar=eaflo[c][:, hd:hd + 1],
                    in1=maskb[:, dr * 128:(dr + 1) * 128], op0=ALU.mult, op1=ALU.mult),
                   reads=[("P", 3, hd), "maskb", ("eaflo", c)], writes=[("SD", di)])
                op("act", lambda e: e.activation(
                    out=kp[ki][:, :], in_=ktok[c][:, hd * HD:(hd + 1) * HD], func=AF.Identity,
                    scale=eaflo[c][:, hd:hd + 1]),
                   reads=[("ktok", c), ("eaflo", c)], writes=[("kp", ki)])
                c0 = hd * 2 * 257
                for j in range(2):
                    op("act", lambda e, j=j: e.activation(
                        out=Cbf[:, c0 + j * 257:c0 + (j + 1) * 257], in_=Cst[:, c0 + j * 257:c0 + (j + 1) * 257],
                        func=AF.Identity, scale=decb[c][:, hd:hd + 1]),
                       reads=[("C", hd, j), ("decb", c)], writes=[("Cbf", hd, j)])

            def mm_n(hd):
                di, ki, nb = st[hd]
                va = vaug[c][:, hd * 257:(hd + 1) * 257]
                c0 = hd * 2 * 257
                op("pe", lambda e: e.matmul(ps[nb][:, 0:257], lhsT=SD[di][:, :], rhs=va, start=True, stop=False),
                   reads=[("SD", di), ("vaug", c)], writes=[("P", nb, "np")])
                for j in range(2):
                    op("pe", lambda e, j=j: e.matmul(
                        ps[nb][:, 0:257], lhsT=qT3[:, 2 * hd + j, cs], rhs=Cbf[:, c0 + j * 257:c0 + (j + 1) * 257],
                        start=False, stop=(j == 1)),
                       reads=[("qT", 2 * hd + j), ("Cbf", hd, j)], writes=[("P", nb, "np")])

            def post(hd):
                di, ki, nb = st[hd]
                op("dve", lambda e: e.tensor_tensor(
                    out=den[:, hd:hd + 1], in0=ps[nb][:, 256:257], in1=eaflo[c][:, 4 + hd:5 + hd], op=ALU.max),
                   reads=[("P", nb, "np"), ("eaflo", c)], writes=[("den0", hd)])
                op("dve", lambda e: e.scalar_tensor_tensor(
                    out=den[:, hd:hd + 1], in0=ps[nb][:, 256:257], scalar=-1.0, in1=den[:, hd:hd + 1],
                    op0=ALU.mult, op1=ALU.max),
                   reads=[("P", nb, "np"), ("den0", hd)], writes=[("den", hd)])
                op("dve", lambda e: e.reciprocal(out=den[:, 4 + hd:5 + hd], in_=den[:, hd:hd + 1]),
                   reads=[("den", hd)], writes=[("rden", hd)])
                hsl = slice(hd * HD, (hd + 1) * HD)
                if add_bwd:
                    op("dve", lambda e: e.scalar_tensor_tensor(
                        out=hacc[hb_i][:, hsl], in0=ps[nb][:, 0:HD], scalar=den[:, 4 + hd:5 + hd],
                        in1=hacc[hb_i][:, hsl], op0=ALU.mult, op1=ALU.add),
                       reads=[("P", nb, "np"), ("rden", hd), ("hacc", hb_i)], writes=[("hacc", hb_i)])
                else:
                    op("act", lambda e: e.activation(
                        out=hacc[hb_i][:, hsl], in_=ps[nb][:, 0:HD], func=AF.Identity, scale=den[:, 4 + hd:5 + hd]),
                       reads=[("P", nb, "np"), ("rden", hd)], writes=[("hacc", hb_i)])

            def mm_u(hd):
                di, ki, nb = st[hd]
                va = vaug[c][:, hd * 257:(hd + 1) * 257]
                for j in range(2):
                    ub = 6 + j
                    op("pe", lambda e, j=j, ub=ub: e.matmul(
                        ps[ub][:, 0:257], lhsT=kp[ki][:, j * 128:(j + 1) * 128], rhs=va, start=True, stop=True),
                       reads=[("kp", ki), ("vaug", c)], writes=[("P", ub, "up")])

            def upd(hd):
                c0 = hd * 2 * 257
                for j in range(2):
                    ub = 6 + j
                    op("dve", lambda e, j=j, ub=ub: e.scalar_tensor_tensor(
                        out=Cst[:, c0 + j * 257:c0 + (j + 1) * 257], in0=Cst[:, c0 + j * 257:c0 + (j + 1) * 257],
                        scalar=decb[c][:, hd:hd + 1], in1=ps[ub][:, 0:257], op0=ALU.mult, op1=ALU.add),
                       reads=[("C", hd, j), ("decb", c), ("P", ub, "up")], writes=[("C", hd, j)])

            pre(0)
            mm_n(0)
            for hd in range(NH):
                if hd + 1 < NH:
                    pre(hd + 1)
                mm_u(hd)
                if hd + 1 < NH:
                    mm_n(hd + 1)
                post(hd)
                upd(hd)

        def seq_reset():
            op("pool", lambda e: e.memset(Cst[:, :], 0.0),
               reads=[("C", h_, j) for h_ in range(NH) for j in range(2)],
               writes=[("C", h_, j) for h_ in range(NH) for j in range(2)])
            op("dve", lambda e: e.memset(mst[:, 0:1], 0.0), reads=["mstate"], writes=["mstate"])

        hbc = {"n": 0}
        cur = 0
        for l in range(L):
            nxt = 1 - cur
            tilesA = []
            for si in reversed(range(S)):
                s0, ln = seqs[si]
                for t0 in reversed(range(s0, s0 + ln, T)):
                    tilesA.append(dict(si=si, s0=s0, ln=ln, t0=t0, newseq=(t0 + T == s0 + ln)))
            tilesB = []
            for si in range(S):
                s0, ln = seqs[si]
                for t0 in range(s0, s0 + ln, T):
                    tilesB.append(dict(si=si, s0=s0, ln=ln, t0=t0, newseq=(t0 == s0)))

            def prep1(tl, idx):
                nonlocal pi
                tl["bi"] = pi % 2
                pi += 1
                tl["hi"] = idx % 2
                tl["first"], tl["last"] = load_x(cur, tl["bi"], tl["s0"], tl["ln"], tl["t0"])
                norm_a(tl["bi"])

            def prep2(tl, halo):
                norm_b(l, tl["si"], tl["bi"], tl["first"], tl["last"], halo, tl["hi"])

            for idx, tl in enumerate(tilesA):
                if idx == 0:
                    prep1(tl, 0)
                    prep2(tl, False)
                set_h(tl["hi"])
                if tl["newseq"]:
                    seq_reset()
                t0 = tl["t0"]
                phase2(l, 1, list(reversed(range(NCH))), False)
                for ci, c in enumerate(reversed(range(NCH))):
                    hb_i = hbc["n"] % 2
                    hbc["n"] += 1
                    mlstm_chunk(1, c, hb_i, add_bwd=False)
                    r0 = t0 + c * CH
                    dma("sp", ("hacc", hb_i), lambda e, hb_i=hb_i, r0=r0: e.dma_start(
                        out=hbd[r0:r0 + CH, :], in_=hacc[hb_i][:, :]),
                        reads=[("hacc", hb_i)], writes=[("hb", r0 // CH)])
                    if idx + 1 < len(tilesA):
                        if ci == 0:
                            prep1(tilesA[idx + 1], idx + 1)
                        if ci == 1:
                            prep2(tilesA[idx + 1], False)
            for idx, tl in enumerate(tilesB):
                if True:
                    if idx == 0:
                        prep1(tl, 0)
                        prep2(tl, True)
                    set_h(tl["hi"])
                    if tl["newseq"]:
                        seq_reset()
                    t0, si, bi = tl["t0"], tl["si"], tl["bi"]
                    phase2(l, 0, list(range(NCH)), True)
                    dump("GT", GT[:, :], KC * T, [("GT", k) for k in range(KC)])
                    for c in range(NCH):
                        hb_i = hbc["n"] % 2
                        hbc["n"] += 1
                        r0 = t0 + c * CH
                        dma("sp", ("hacc", hb_i), lambda e, hb_i=hb_i, r0=r0: e.dma_start(
                            out=hacc[hb_i][:, :], in_=hbd[r0:r0 + CH, :]),
                            reads=[("hb", r0 // CH)], writes=[("hacc", hb_i)])
                        if c == 0:
                            dump("hbw0", hacc[hb_i][:, :], D, [("hacc", hb_i)])
                        mlstm_chunk(0, c, hb_i, add_bwd=True)
                        if c == 0:
                            dump("hsum0", hacc[hb_i][:, :], D, [("hacc", hb_i)])
                            dump("eaflo0", eaflo[0][:, :], 8, [("eaflo", 0)])
                            dump("decb0", decb[0][:, :], 4, [("decb", 0)])
                        if c == 1:
                            dump("hsum1", hacc[hb_i][:, :], D, [("hacc", hb_i)])
                            dump("eaflo1", eaflo[1][:, :], 8, [("eaflo", 1)])
                            dump("decb1", decb[1][:, :], 4, [("decb", 1)])
                        for hd in range(NH):
                            hsl = slice(hd * HD, (hd + 1) * HD)
                            op("act", lambda e, hsl=hsl, hd=hd, hb_i=hb_i: e.activation(
                                out=junk[:, :], in_=hacc[hb_i][:, hsl], func=AF.Square,
                                accum_out=ssq[:, hd:hd + 1]),
                               reads=[("hacc", hb_i)], writes=["junk", ("ssq", hd)])
                        op("dve", lambda e: e.tensor_scalar(
                            out=ssq[:, 4:8], in0=ssq[:, 0:4], scalar1=1.0 / HD, scalar2=EPS, op0=ALU.mult,
                            op1=ALU.add), reads=[("ssq", h_) for h_ in range(NH)], writes=["ssq2"])
                        op("act", lambda e: e.activation(out=ssq[:, 4:8], in_=ssq[:, 4:8], func=AF.Sqrt),
                           reads=["ssq2"], writes=["ssq2b"])
                        op("dve", lambda e: e.reciprocal(out=ssq[:, 4:8], in_=ssq[:, 4:8]),
                           reads=["ssq2b"], writes=["ssq3"])
                        for hd in range(NH):
                            hsl = slice(hd * HD, (hd + 1) * HD)
                            if hd % 2:
                                op("act", lambda e, hsl=hsl, hd=hd, hb_i=hb_i: e.activation(
                                    out=hacc[hb_i][:, hsl], in_=hacc[hb_i][:, hsl], func=AF.Identity,
                                    scale=ssq[:, 4 + hd:5 + hd]),
                                   reads=[("hacc", hb_i), "ssq3"], writes=[("hacc", hb_i)])
                            else:
                                op("dve", lambda e, hsl=hsl, hd=hd, hb_i=hb_i: e.tensor_scalar(
                                    out=hacc[hb_i][:, hsl], in0=hacc[hb_i][:, hsl], scalar1=ssq[:, 4 + hd:5 + hd],
                                    scalar2=None, op0=ALU.mult),
                                   reads=[("hacc", hb_i), "ssq3"], writes=[("hacc", hb_i)])
                        for half in range(2):
                            bank = next_bank()
                            for jj in range(4):
                                j = half * 4 + jj
                                op("pe", lambda e, jj=jj, j=j, bank=bank, hb_i=hb_i: e.transpose(
                                    ps[bank][:, jj * 128:(jj + 1) * 128], hacc[hb_i][:, j * 128:(j + 1) * 128], ident),
                                   reads=[("hacc", hb_i), "cst"], writes=[("P", bank, 0)])
                            op("dve", lambda e, half=half, bank=bank, c=c: e.tensor_tensor(
                                out=ymT3[:, half * 4:half * 4 + 4, c * CH:(c + 1) * CH],
                                in0=ps[bank][:, :].rearrange("p (j t) -> p j t", t=128),
                                in1=GT3[:, half * 4:half * 4 + 4, c * CH:(c + 1) * CH], op=ALU.mult),
                               reads=[("P", bank, 0)] + [("GT", half * 4 + jj) for jj in range(4)],
                               writes=[("ymT", half * 4 + jj) for jj in range(4)])
                        if idx + 1 < len(tilesB):
                            if c == 0:
                                prep1(tilesB[idx + 1], idx + 1)
                            if c == 1:
                                prep2(tilesB[idx + 1], True)
                    dump("ymT", ymT[:, :], KC * T, [("ymT", k) for k in range(KC)], bf=True)
                    cwo = l * PP_L + PP_CW
                    cbo = l * PP_L + PP_CB
                    for j in range(8):
                        i2 = j % 2

                        def halo_mm(slot, col):
                            for k in range(KC):
                                op("pe", lambda e, k=k, slot=slot, col=col: e.matmul(
                                    ps[6][:, col:col + 2], lhsT=wr3[slot][:, k, :], rhs=H["v"][:, k, 0:XW:XW - 1],
                                    start=(k == 0), stop=(k == KC - 1)),
                                   reads=[("w", slot)] + hT_all, writes=[("P", 6, col)])
                        b_cc = proj_block(l, B_CC + j, rhs_h, hT_all, extra=lambda slot: halo_mm(slot, 304))
                        op("act", lambda e, b_cc=b_cc, i2=i2: e.activation(out=ubuf[i2][:, 1:T + 1], in_=ps[b_cc][:, :],
                                                                             func=AF.Copy),
                           reads=[("P", b_cc, 0)], writes=[("ubuf", i2)])
                        op("act", lambda e: e.activation(out=cch[:, :], in_=ps[6][:, 304:306], func=AF.Copy),
                           reads=[("P", 6, 304)], writes=["cch"])
                        b_cx = proj_block(l, B_CX + j, rhs_h, hT_all, extra=lambda slot: halo_mm(slot, 308))
                        op("dve", lambda e, b_cx=b_cx, i2=i2: e.tensor_tensor(
                            out=ubuf[i2][:, 1:T + 1], in0=ps[b_cx][:, :], in1=ubuf[i2][:, 1:T + 1], op=ALU.mult),
                           reads=[("P", b_cx, 0), ("ubuf", i2)], writes=[("ubuf", i2)])
                        op("dve", lambda e, i2=i2: e.tensor_tensor(
                            out=ubuf[i2][:, 0:XW:XW - 1], in0=ps[6][:, 308:310], in1=cch[:, :], op=ALU.mult),
                           reads=[("P", 6, 308), "cch", ("ubuf", i2)], writes=[("ubuf", i2)])
                        b_cz = proj_block(l, B_CZ + j, rhs_h, hT_all)
                        op("act", lambda e, b_cz=b_cz, i2=i2: e.activation(out=csz[i2][:, :], in_=ps[b_cz][:, :],
                                                                             func=AF.Silu),
                           reads=[("P", b_cz, 0)], writes=[("csz", i2)])
                        b_cb = proj_block(l, B_CB + j, rhs_h, hT_all)
                        dump("u0", ubuf[i2][:, :], XW, [("ubuf", i2)])
                        op("act", lambda e, i2=i2, j=j: e.activation(
                            out=ct1[i2][:, :], in_=ubuf[i2][:, 1:T + 1], func=AF.Identity,
                            scale=pp[:, cwo + 8 + j:cwo + 9 + j], bias=pp[:, cbo + j:cbo + j + 1]),
                           reads=[("ubuf", i2), "pp"], writes=[("ct1", i2)])
                        dump("c1", ct1[i2][:, :], T, [("ct1", i2)])
                        op("dve", lambda e, i2=i2, j=j: e.scalar_tensor_tensor(
                            out=ct1[i2][:, :], in0=ubuf[i2][:, 0:T], scalar=pp[:, cwo + j:cwo + j + 1],
                            in1=ct1[i2][:, :], op0=ALU.mult, op1=ALU.add),
                           reads=[("ubuf", i2), "pp", ("ct1", i2)], writes=[("ct1", i2)])
                        op("dve", lambda e, i2=i2, j=j: e.scalar_tensor_tensor(
                            out=ct1[i2][:, :], in0=ubuf[i2][:, 2:T + 2], scalar=pp[:, cwo + 16 + j:cwo + 17 + j],
                            in1=ct1[i2][:, :], op0=ALU.mult, op1=ALU.add),
                           reads=[("ubuf", i2), "pp", ("ct1", i2)], writes=[("ct1", i2)])
                        dump("c3", ct1[i2][:, :], T, [("ct1", i2)])
                        op("dve", lambda e, b_cb=b_cb, i2=i2: e.tensor_tensor(
                            out=ct1[i2][:, :], in0=ps[b_cb][:, :], in1=ct1[i2][:, :], op=ALU.mult),
                           reads=[("P", b_cb, 0), ("ct1", i2)], writes=[("ct1", i2)])
                        dump("c4", ct1[i2][:, :], T, [("ct1", i2)])
                        dump("csz", csz[i2][:, :], T, [("csz", i2)])
                        op("dve", lambda e, i2=i2, j=j: e.tensor_tensor(
                            out=ycT3[:, j, :], in0=ct1[i2][:, :], in1=csz[i2][:, :], op=ALU.mult),
                           reads=[("ct1", i2), ("csz", i2)], writes=[("vT", j)])
                    dump("ycT", ycT[:, :], KC * T, [("vT", k) for k in range(KC)], bf=True)
                    yc_all = [("vT", k) for k in range(KC)]
                    ym_all = [("ymT", k) for k in range(KC)]
                    for j in range(8):
                        i2 = j % 2
                        b_gc = proj_block(l, B_GC + j, rhs_h, hT_all)
                        op("act", lambda e, b_gc=b_gc, i2=i2: e.activation(out=sg1[i2][:, :], in_=ps[b_gc][:, :],
                                                                             func=AF.Sigmoid),
                           reads=[("P", b_gc, 0)], writes=[("sg1", i2)])
                        b_gm = proj_block(l, B_GM + j, rhs_h, hT_all)
                        op("act", lambda e, b_gm=b_gm, i2=i2: e.activation(out=sg2[i2][:, :], in_=ps[b_gm][:, :],
                                                                             func=AF.Sigmoid),
                           reads=[("P", b_gm, 0)], writes=[("sg2", i2)])
                        b_pc = proj_block(l, B_PC + j, lambda k: ycT3[:, k, :], yc_all)
                        op("dve", lambda e, b_pc=b_pc, i2=i2: e.tensor_tensor(
                            out=sg1[i2][:, :], in0=ps[b_pc][:, :], in1=sg1[i2][:, :], op=ALU.mult),
                           reads=[("P", b_pc, 0), ("sg1", i2)], writes=[("sg1", i2)])
                        b_pm = proj_block(l, B_PM + j, lambda k: ymT3[:, k, :], ym_all)
                        op("dve", lambda e, b_pm=b_pm, i2=i2: e.tensor_tensor(
                            out=sg2[i2][:, :], in0=ps[b_pm][:, :], in1=sg2[i2][:, :], op=ALU.mult),
                           reads=[("P", b_pm, 0), ("sg2", i2)], writes=[("sg2", i2)])
                        op("dve", lambda e, i2=i2, j=j: e.tensor_tensor(
                            out=mgT3[:, j, :], in0=sg1[i2][:, :], in1=sg2[i2][:, :], op=ALU.add),
                           reads=[("sg1", i2), ("sg2", i2)], writes=[("qT", j)])
                    dump("mgT", mgT[:, :], KC * T, [("qT", k) for k in range(KC)], bf=True)
                    mg_all = [("qT", k) for k in range(KC)]
                    o0 = (l * S + si) * 24
                    for j in range(8):
                        b_o = proj_block(l, B_WO + j, lambda k: mgT3[:, k, :], mg_all)
                        i2 = j % 2
                        op("dve", lambda e, b_o=b_o, i2=i2, j=j: e.scalar_tensor_tensor(
                            out=ct1[i2][:, :], in0=ps[b_o][:, :], scalar=modsb[:, o0 + 16 + j:o0 + 17 + j],
                            in1=xt3[bi][:, j, 1:T + 1], op0=ALU.mult, op1=ALU.add),
                           reads=[("P", b_o, 0), "modsb", ("xt", bi)], writes=[("ct1", i2)])
                        dma("sp", ("ct1", i2), lambda e, i2=i2, j=j, t0=t0: e.dma_start(
                            out=xTd[nxt][j * 128:(j + 1) * 128, t0:t0 + T], in_=ct1[i2][:, :]),
                            reads=[("ct1", i2)], writes=[("xT", nxt, t0 // T, j)])
            cur = nxt

        ep = []
        for ti in range(ntile_all):
            for c in range(NCH):
                ep.append((ti, c))
        ep_bi = {}

        def ep_load(ti):
            nonlocal pi
            t0 = ti * T
            bi = pi % 2
            pi += 1
            ep_bi[ti] = bi
            dma("act", ("xt", bi), lambda e: e.dma_start(
                out=xt3[bi][:, :, 0:T], in_=xTd[cur][:, t0:t0 + T].rearrange("(k p) t -> p k t", p=128)),
                reads=[("xT", cur, ti, k_) for k_ in range(KC)], writes=[("xt", bi)])

        def ep_s1(n):
            ti, c = ep[n]
            t0 = ti * T
            if c == 1 and ti + 1 < ntile_all:
                ep_load(ti + 1)
            bi = ep_bi[ti]
            hb_i = n % 3
            for half in range(2):
                bank = next_bank()
                for jj in range(4):
                    k = half * 4 + jj
                    op("pe", lambda e, jj=jj, k=k: e.transpose(
                        ps[bank][:, jj * 128:(jj + 1) * 128], xt3[bi][:, k, c * CH:(c + 1) * CH], ident),
                       reads=[("xt", bi), "cst"], writes=[("P", bank, 0)])
                if half == 0:
                    op("dve", lambda e: e.tensor_copy(out=hacc[hb_i][:, 0:512], in_=ps[bank][:, :]),
                       reads=[("P", bank, 0)], writes=[("hacc", hb_i)])
                else:
                    op("act", lambda e: e.activation(out=hacc[hb_i][:, 512:1024], in_=ps[bank][:, :], func=AF.Copy),
                       reads=[("P", bank, 0)], writes=[("hacc", hb_i)])

        def ep_s2(n):
            ti, c = ep[n]
            t0 = ti * T
            hb_i = n % 3
            o = 8 * (n % 2)
            for q_ in range(4):
                op("act", lambda e, q_=q_: e.activation(
                    out=junk[:, :], in_=hacc[hb_i][:, q_ * HD:(q_ + 1) * HD], func=AF.Square,
                    accum_out=ssqe[:, o + q_:o + q_ + 1]), reads=[("hacc", hb_i)], writes=["junk", ("ssqe", o, q_)])
            op("dve", lambda e: e.tensor_reduce(out=ssqe[:, o + 4:o + 5], in_=ssqe[:, o:o + 4], axis=AX.X, op=ALU.add),
               reads=[("ssqe", o, q_) for q_ in range(4)], writes=[("ssqe", o, 4)])
            op("dve", lambda e: e.tensor_scalar(out=ssqe[:, o + 5:o + 6], in0=ssqe[:, o + 4:o + 5], scalar1=1.0 / D,
                                                scalar2=EPS, op0=ALU.mult, op1=ALU.add),
               reads=[("ssqe", o, 4)], writes=[("ssqe", o, 5)])
            op("act", lambda e: e.activation(out=ssqe[:, o + 6:o + 7], in_=ssqe[:, o + 5:o + 6], func=AF.Sqrt),
               reads=[("ssqe", o, 5)], writes=[("ssqe", o, 6)])
            op("dve", lambda e: e.reciprocal(out=ssqe[:, o + 7:o + 8], in_=ssqe[:, o + 6:o + 7]),
               reads=[("ssqe", o, 6)], writes=[("ssqe", o, 7)])
            op("dve", lambda e: e.scalar_tensor_tensor(
                out=hacc[hb_i][:, :], in0=hacc[hb_i][:, :], scalar=ssqe[:, o + 7:o + 8], in1=fg[:, :],
                op0=ALU.mult, op1=ALU.mult), reads=[("hacc", hb_i), ("ssqe", o, 7), "fg"], writes=[("hacc", hb_i)])
            r0 = t0 + c * CH
            dma("sp", ("hacc", hb_i), lambda e: e.dma_start(
                out=yout[r0:r0 + CH, :], in_=hacc[hb_i][:, :]), reads=[("hacc", hb_i)], writes=[("y", r0)])

        ep_load(0)
        ep_s1(0)
        for n in range(len(ep)):
            if n + 1 < len(ep):
                ep_s1(n + 1)
            ep_s2(n)

        keys = list(S_.count.keys())
        sems = {}
        for i, k in enumerate(keys):
            sems[k] = es.enter_context(nc.semaphore(f"s{i}"))
        final_waits = [(k, S_.count[k]) for k in keys]
        eng_objs = {"pe": "tensor", "act": "scalar", "dve": "vector", "pool": "gpsimd", "sp": "sync"}
        with nc.Block() as block:
            def make(engname):
                def body(e):
                    for waits, fn, inckey, amt in S_.ops[engname]:
                        for k, v in waits:
                            e.wait_ge(sems[k], v)
                        fn(e).then_inc(sems[inckey], amt)
                    if engname == "sp":
                        for k, v in final_waits:
                            e.wait_ge(sems[k], v)
                return body
            for en, attr in eng_objs.items():
                getattr(block, attr)(make(en))
    return nc


def _host_layout(inputs, depth, core_seqs):
    L = depth
    if L == 0:
        shared0 = {"winb": np.zeros((1, NBLK, 128, KC * 128), np.float32), "wgin": np.zeros((1, 128, KC * 16), np.float32),
                   "wada": np.zeros((1, 128, KC, 3 * D), np.float32), "pp": np.zeros((128, PP_L), np.float32),
                   "bg": np.zeros((128, 16), np.float32)}
    w_in = np.asarray(inputs["w_in"], np.float32)[:max(L, 1)]
    colblocks = list(range(0, 72)) + [None] * 0
    blocks = []
    for l in range(L):
        lay = []
        wi = w_in[l]
        cols = np.concatenate([wi[:, 0:9216], wi[:, 9232:11280]], axis=1)
        mats = [cols, np.asarray(inputs["w_proj_conv"][l]), np.asarray(inputs["w_proj_mlstm"][l]),
                np.asarray(inputs["w_out"][l])]
        allc = np.concatenate(mats, axis=1)
        blk = allc.reshape(KC, 128, NBLK, 128).transpose(2, 1, 0, 3)
        blocks.append(blk.reshape(NBLK, 128, KC * 128))
    winb = np.ascontiguousarray(np.stack(blocks, 0), dtype=np.float32) if L else None
    wgin = np.ascontiguousarray(
        w_in[:, :, 9216:9232].reshape(L, KC, 128, 16).transpose(0, 2, 1, 3).reshape(L, 128, KC * 16))
    wada = np.ascontiguousarray(
        np.asarray(inputs["w_ada"], np.float32)[:L].reshape(L, KC, 128, 3 * D).transpose(0, 2, 1, 3))

    def fT(v, n):
        return np.asarray(v, np.float32).reshape(n, 128).T

    pp = np.zeros((128, L * PP_L), np.float32)
    for l in range(L):
        o = l * PP_L
        pp[:, o + PP_BADA:o + PP_BADA + 24] = fT(inputs["b_ada"][l], 24)
        pp[:, o + PP_NG:o + PP_NG + 8] = fT(inputs["norm_g"][l], 8)
        for tap in range(3):
            pp[:, o + PP_CW + tap * 8:o + PP_CW + tap * 8 + 8] = fT(inputs["conv_w"][l][tap], 8)
        pp[:, o + PP_CB:o + PP_CB + 8] = fT(inputs["conv_b"][l], 8)
        pp[:, o + PP_MG:o + PP_MG + 8] = fT(inputs["mh_norm_g"][l], 8)
    bg = np.ascontiguousarray(np.broadcast_to(
        np.asarray(inputs["b_gates"], np.float32)[:L].reshape(1, L * 16), (128, L * 16)))
    fg = np.ascontiguousarray(np.broadcast_to(np.asarray(inputs["final_norm_g"], np.float32)[None, :], (128, D)))
    ii = np.arange(128)
    cst = np.concatenate([np.eye(128, dtype=np.float32),
                          (ii[:, None] <= ii[None, :]).astype(np.float32),
                          (ii[:, None] >= ii[None, :]).astype(np.float32)], axis=1)
    shared = {"winb": winb, "wgin": wgin, "wada": wada, "pp": pp, "bg": bg, "fg": fg, "cst": cst}
    if L == 0:
        shared.update(shared0)
    in_maps = []
    for seqs in core_seqs:
        xs, cs = [], []
        for which, b in seqs:
            xs.append(np.asarray(inputs["x_" + which][b], np.float32))
            cs.append(np.asarray(inputs["c_" + which][b], np.float32))
        xin = np.ascontiguousarray(np.concatenate(xs, axis=0))
        c = np.stack(cs, 0)
        cT = np.ascontiguousarray(c.reshape(len(cs), KC, 128).transpose(2, 1, 0).reshape(128, KC * len(cs)))
        m = dict(shared)
        m["xin"] = xin
        m["cT"] = cT
        in_maps.append(m)
    return in_maps


def _run(inputs, depth, n_cores=8, trace=False):
    xp = np.asarray(inputs["x_prompt"])
    xs = np.asarray(inputs["x_sample"])
    nbp, lp = xp.shape[0], xp.shape[1]
    nbs, ls = xs.shape[0], xs.shape[1]
    pp_core, ps_core = nbp // n_cores, nbs // n_cores
    core_seqs = []
    for c in range(n_cores):
        core_seqs.append([("prompt", c * pp_core + i) for i in range(pp_core)] +
                         [("sample", c * ps_core + i) for i in range(ps_core)])
    seq_lens = [lp] * pp_core + [ls] * ps_core
    nc = _build(seq_lens, depth)
    in_maps = _host_layout(inputs, depth, core_seqs)
    res = run_bass_kernel_spmd(nc, in_maps, core_ids=list(range(n_cores)), **({"trace": True} if trace else {}))
    yp = np.zeros(xp.shape, np.float32)
    ys = np.zeros(xs.shape, np.float32)
    for c in range(n_cores):
        y = res.results[c]["y"]
        o = 0
        for which, b in core_seqs[c]:
            ln = lp if which == "prompt" else ls
            (yp if which == "prompt" else ys)[b] = y[o:o + ln]
            o += ln
    return (yp, ys), res


def kernel(**inputs):
    out, _ = _run(inputs, depth=4)
    return out
```
